# Optimizing a Trainium2 kernel written in Bass

```python
import math
import jax, jax.numpy as jnp
from jax import lax
import numpy as np

D_MODEL = 1024
BATCH = 8
SEQ = 2048
DEPTH = 1
DEC_BATCH = 128
DEC_SEQ = 8
PAST_LEN = 16384
PAGE_SIZE = 128

MIX_WIDTH = D_MODEL
RET_WIDTH = MIX_WIDTH // 2
RET_HEADS = 4
RET_HD = RET_WIDTH // RET_HEADS
CONV_WIDTH = MIX_WIDTH - RET_WIDTH
CONV_GROUPS = 4
CONV_K = 3
RET_CHUNK = 128
ROPE_BASE = 10000.0
D_FF = ((8 * D_MODEL + 3 * 256 - 1) // (3 * 256)) * 256
IN_COLS = 4 * RET_WIDTH + 3 * CONV_WIDTH
NORM_EPS = 1e-6
GN_EPS = 1e-5

kernel_name = "retnet_shortconv_hybrid_step"


def rmsnorm(x, g):
    x32 = x.astype(jnp.float32)
    y = x32 * lax.rsqrt(jnp.mean(x32 * x32, axis=-1, keepdims=True) + NORM_EPS)
    return (y * g.astype(jnp.float32)).astype(x.dtype)


def rotary(t, pos):
    half = t.shape[-1] // 2
    inv = ROPE_BASE ** (-jnp.arange(half, dtype=jnp.float32) / half)
    ang = pos[:, None] * inv[None, :]
    cos = jnp.cos(ang)[None, :, None, :]
    sin = jnp.sin(ang)[None, :, None, :]
    t32 = t.astype(jnp.float32)
    t1, t2 = t32[..., :half], t32[..., half:]
    return jnp.concatenate([t1 * cos - t2 * sin, t1 * sin + t2 * cos], axis=-1)


def retention_chunkwise(q, k, v, state0):
    B, L, H, D = q.shape
    C = math.gcd(L, RET_CHUNK)
    n = L // C
    log_g = jnp.log1p(-jnp.exp2(-5.0 - jnp.arange(H, dtype=jnp.float32)))
    idx = jnp.arange(C, dtype=jnp.float32)
    diff = idx[:, None] - idx[None, :]
    causal = diff >= 0
    decay = jnp.where(causal[None], jnp.exp(jnp.where(causal, diff, 0.0)[None] * log_g[:, None, None]), 0.0)
    xi = jnp.exp((idx[None, :] + 1.0) * log_g[:, None])
    zeta = jnp.exp((C - 1.0 - idx[None, :]) * log_g[:, None])
    g_chunk = jnp.exp(C * log_g)

    def to_chunks(t):
        return t.reshape(B, n, C, H, D).transpose(1, 0, 3, 2, 4)

    def step(R, xs):
        qc, kc, vc = xs
        scores = jnp.einsum('bhid,bhjd->bhij', qc, kc) * decay[None]
        o = jnp.einsum('bhij,bhjv->bhiv', scores, vc)
        o = o + jnp.einsum('bhid,bhdv->bhiv', qc, R) * xi[None, :, :, None]
        R_new = R * g_chunk[None, :, None, None] + jnp.einsum(
            'bhjd,bhjv->bhdv', kc * zeta[None, :, :, None], vc)
        return R_new, o

    R_fin, o = lax.scan(step, state0, (to_chunks(q), to_chunks(k), to_chunks(v)))
    o = o.transpose(1, 0, 3, 2, 4).reshape(B, L, H, D)
    return o, R_fin


def layer(x, conv_state, ret_state, pos0, norm1_g, w_in, conv_w, ret_gn_g, w_out,
          norm2_g, w_gate, w_up, w_down):
    B, L, _ = x.shape
    h = rmsnorm(x, norm1_g)
    proj = h @ w_in
    splits = np.cumsum([RET_WIDTH] * 4 + [CONV_WIDTH] * 2).tolist()
    q, k, v, g, bg, cg, xt = jnp.split(proj, splits, axis=-1)

    pos = pos0 + jnp.arange(L, dtype=jnp.float32)
    qh = rotary(q.reshape(B, L, RET_HEADS, RET_HD), pos)
    kh = rotary(k.reshape(B, L, RET_HEADS, RET_HD), pos) * (RET_HD ** -0.5)
    vh = v.reshape(B, L, RET_HEADS, RET_HD).astype(jnp.float32)
    o, R_new = retention_chunkwise(qh, kh, vh, ret_state.astype(jnp.float32))
    mu = jnp.mean(o, axis=-1, keepdims=True)
    var = jnp.mean(jnp.square(o - mu), axis=-1, keepdims=True)
    o = ((o - mu) * lax.rsqrt(var + GN_EPS)).reshape(B, L, RET_WIDTH) * ret_gn_g.astype(jnp.float32)
    o_ret = (jax.nn.silu(g.astype(jnp.float32)) * o).astype(x.dtype)

    u = cg * xt
    ext = jnp.concatenate([conv_state.astype(u.dtype), u], axis=1)
    conv_y = sum(conv_w[j] * ext[:, j:j + L] for j in range(CONV_K))
    conv_new = ext[:, L:]
    o_conv = bg * conv_y

    x = x + jnp.concatenate([o_ret, o_conv.astype(x.dtype)], axis=-1) @ w_out

    h2 = rmsnorm(x, norm2_g)
    x = x + (jax.nn.silu(h2 @ w_gate) * (h2 @ w_up)) @ w_down
    return x, conv_new, R_new


def setup_inputs(seed: int = 0) -> dict:
    key = jax.random.key(seed)
    ks = jax.random.split(key, 16)
    f32 = jnp.float32
    nrm = lambda k, s, sc: jax.random.normal(k, s, f32) * sc
    return {
        "x_prompt": nrm(ks[0], (BATCH, SEQ, D_MODEL), 1.0),
        "x_sample": nrm(ks[1], (DEC_BATCH, DEC_SEQ, D_MODEL), 1.0),
        "state_conv": nrm(ks[2], (DEPTH, DEC_BATCH, CONV_K - 1, CONV_WIDTH), 1.0),
        "state_ret": nrm(ks[3], (DEPTH, DEC_BATCH, RET_HEADS, RET_HD, RET_HD), 0.5),
        "norm1_g": 1.0 + nrm(ks[4], (DEPTH, D_MODEL), 0.01),
        "w_in": nrm(ks[5], (DEPTH, D_MODEL, IN_COLS), D_MODEL ** -0.5),
        "conv_w": nrm(ks[6], (DEPTH, CONV_K, CONV_WIDTH), 0.5),
        "ret_gn_g": 1.0 + nrm(ks[7], (DEPTH, RET_WIDTH), 0.01),
        "w_out": nrm(ks[8], (DEPTH, MIX_WIDTH, D_MODEL), MIX_WIDTH ** -0.5),
        "norm2_g": 1.0 + nrm(ks[9], (DEPTH, D_MODEL), 0.01),
        "w_gate": nrm(ks[10], (DEPTH, D_MODEL, D_FF), D_MODEL ** -0.5),
        "w_up": nrm(ks[11], (DEPTH, D_MODEL, D_FF), D_MODEL ** -0.5),
        "w_down": nrm(ks[12], (DEPTH, D_FF, D_MODEL), D_FF ** -0.5),
        "norm_f_g": 1.0 + nrm(ks[13], (D_MODEL,), 0.01),
    }


def reference(x_prompt, x_sample, state_conv, state_ret, norm1_g, w_in, conv_w, ret_gn_g,
              w_out, norm2_g, w_gate, w_up, w_down, norm_f_g):
    hp, hs = x_prompt, x_sample
    cp_list, rp_list, cs_list, rs_list = [], [], [], []
    for l in range(DEPTH):
        params = (norm1_g[l], w_in[l], conv_w[l], ret_gn_g[l], w_out[l],
                  norm2_g[l], w_gate[l], w_up[l], w_down[l])
        conv0 = jnp.zeros((BATCH, CONV_K - 1, CONV_WIDTH), x_prompt.dtype)
        ret0 = jnp.zeros((BATCH, RET_HEADS, RET_HD, RET_HD), jnp.float32)
        hp, cp, rp = layer(hp, conv0, ret0, 0.0, *params)
        hs, cs, rs = layer(hs, state_conv[l], state_ret[l], float(PAST_LEN), *params)
        cp_list.append(cp); rp_list.append(rp); cs_list.append(cs); rs_list.append(rs)
    y_prompt = rmsnorm(hp, norm_f_g)
    y_sample = rmsnorm(hs, norm_f_g)
    new_conv_prompt = jnp.stack(cp_list)
    new_ret_prompt = jnp.stack(rp_list)
    new_conv_sample = jnp.stack(cs_list)
    new_ret_sample = jnp.stack(rs_list)
    return (y_prompt, y_sample, new_conv_prompt, new_ret_prompt, new_conv_sample, new_ret_sample)
```

```python
import types
import numpy as np
from contextlib import ExitStack
import concourse.bass as bass
import concourse.mybir as mybir
from concourse.bass_utils import run_bass_kernel_spmd

F32 = mybir.dt.float32
BF16 = mybir.dt.bfloat16
AF = mybir.ActivationFunctionType
ALU = mybir.AluOpType

D = 1024
SEQ = 2048
NCORES = 8
DEC_PER = 16
DEC_SEQ = 8
PAST_LEN = 16384
H = 4
HD = 128
CW = 512
DFF = 2816
NFC = 22
INC = 3584
NT = 17
EPS = 1e-6
GN_EPS = 1e-5
NPAN_IN = 8
NPAN_GU = 11
PANW = 4096


def _freeze(fn):
    if fn.__closure__ is None:
        return fn
    cells = []
    for c in fn.__closure__:
        try:
            cells.append(types.CellType(c.cell_contents))
        except ValueError:
            cells.append(c)
    return types.FunctionType(fn.__code__, fn.__globals__, fn.__name__, fn.__defaults__, tuple(cells))


class Eng:
    def __init__(self, name):
        self.name = name
        self.ops = []
        self.count = 0
        self.waited = {}
        self.sem = None

    def wait(self, key, val):
        if key is None or val is None or val <= 0:
            return
        k = id(key)
        if self.waited.get(k, 0) >= val:
            return
        self.waited[k] = val
        self.ops.append(("wait", (key, val)))


class DmaSem:
    def __init__(self, name):
        self.name = name
        self.count = 0
        self.sem = None


class Buf:
    def __init__(self, name, excl=False):
        self.name = name
        self.w = None
        self.r = {}
        self.excl = excl

    def rdeps(self):
        d = [self.w] if self.w else []
        if self.excl:
            d += list(self.r.values())
        return d

    def wdeps(self):
        d = [self.w] if self.w else []
        return d + list(self.r.values())

    def note_r(self, tk):
        self.r[id(tk[0])] = tk

    def note_w(self, tk):
        self.w = tk
        self.r = {}


class Prog:
    def __init__(self):
        self.pe = Eng("pe")
        self.act = Eng("act")
        self.dve = Eng("dve")
        self.pool = Eng("pool")
        self.sp = Eng("sp")
        self.engs = [self.pe, self.act, self.dve, self.pool, self.sp]
        self.dsems = []

    def dsem(self, name):
        d = DmaSem(name)
        self.dsems.append(d)
        return d

    def _deps(self, eng, reads, writes, extra):
        deps = list(extra)
        for b in reads:
            deps += b.rdeps()
        for b in writes:
            deps += b.wdeps()
        for (k, v) in deps:
            eng.wait(k, v)

    def op(self, eng, fn, reads=(), writes=(), extra=()):
        self._deps(eng, reads, writes, extra)
        eng.count += 1
        eng.ops.append(("op", (_freeze(fn), True)))
        tk = (eng, eng.count)
        for b in reads:
            b.note_r(tk)
        for b in writes:
            b.note_w(tk)
        return tk

    def group(self, eng, fns, reads=(), writes=(), extra=()):
        self._deps(eng, reads, writes, extra)
        for fn in fns[:-1]:
            eng.ops.append(("op", (_freeze(fn), False)))
        eng.count += 1
        eng.ops.append(("op", (_freeze(fns[-1]), True)))
        tk = (eng, eng.count)
        for b in reads:
            b.note_r(tk)
        for b in writes:
            b.note_w(tk)
        return tk

    def dma(self, eng, fn, dsem, reads=(), writes=(), extra=()):
        self._deps(eng, reads, writes, extra)
        dsem.count += 16
        eng.ops.append(("dma", (_freeze(fn), dsem)))
        tk = (dsem, dsem.count)
        for b in reads:
            b.note_r(tk)
        for b in writes:
            b.note_w(tk)
        return tk

    def emit(self, nc, stack):
        for e in self.engs:
            e.sem = stack.enter_context(nc.semaphore("s_" + e.name))
        for d in self.dsems:
            d.sem = stack.enter_context(nc.semaphore("d_" + d.name))
        block = stack.enter_context(nc.Block())

        def run(ir):
            def body(e):
                for kind, payload in ir.ops:
                    if kind == "wait":
                        key, val = payload
                        e.wait_ge(key.sem, val)
                    elif kind == "op":
                        fn, signal = payload
                        ins = fn(e)
                        if signal:
                            ins.then_inc(ir.sem, 1)
                    else:
                        fn, dsem = payload
                        fn(e).then_inc(dsem.sem, 16)
            return body

        block.tensor(run(self.pe))
        block.scalar(run(self.act))
        block.vector(run(self.dve))
        block.gpsimd(run(self.pool))
        block.sync(run(self.sp))


def _const_tables():
    half = HD // 2
    inv = (np.float32(10000.0) ** (-(np.arange(half, dtype=np.float32)) / np.float32(half))).astype(np.float32)
    gam = 1.0 - 2.0 ** (-5.0 - np.arange(H, dtype=np.float64))
    tabs = np.zeros((NT, 128, 1536), np.float32)
    p = np.arange(128)
    for gt in range(NT):
        if gt < 16:
            pos = (128 * gt + p).astype(np.float32)
            loc = p.astype(np.float64)
        else:
            pos = (PAST_LEN + (p % DEC_SEQ)).astype(np.float32)
            loc = (p % DEC_SEQ).astype(np.float64)
        ang = (pos[:, None] * inv[None, :]).astype(np.float32).astype(np.float64)
        c = np.cos(ang)
        s = np.sin(ang)
        sq = gam[None, :] ** (loc[:, None] + 1.0)
        sk = gam[None, :] ** (-(loc[:, None] + 1.0)) * (HD ** -0.5)
        CQ = c[:, None, :] * sq[:, :, None]
        CK = c[:, None, :] * sk[:, :, None]
        SQ = np.stack([-s[:, None, :] * sq[:, :, None], s[:, None, :] * sq[:, :, None]], 2)
        SK = np.stack([-s[:, None, :] * sk[:, :, None], s[:, None, :] * sk[:, :, None]], 2)
        tabs[gt] = np.concatenate([CQ.reshape(128, 256), SQ.reshape(128, 512),
                                   CK.reshape(128, 256), SK.reshape(128, 512)], 1).astype(np.float32)
    j = p[:, None]
    i = p[None, :]
    mask_p = (j <= i).astype(np.float32)
    mask_s = ((j <= i) & (j // DEC_SEQ == i // DEC_SEQ)).astype(np.float32)
    m1 = (p[:, None] // DEC_SEQ == np.arange(DEC_PER)[None, :]).astype(np.float32)
    m2 = np.broadcast_to(m1.T[None, :, :], (128, DEC_PER, 128)).astype(np.float32)
    gC_p = (gam ** 128.0).astype(np.float64)
    gC_s = (gam ** float(DEC_SEQ)).astype(np.float64)
    return tabs, mask_p, mask_s, m1, np.ascontiguousarray(m2), gC_p, gC_s


def build_program(groups=None):
    tabs_np, mask_p_np, mask_s_np, m1_np, m2_np, gC_p, gC_s = _const_tables()
    nc = bass.Bass("TRN2", target_bir_lowering=False)
    P = Prog()

    def din(name, shape, dt=F32):
        return nc.dram_tensor(name, list(shape), dt, kind="ExternalInput").ap()

    def dout(name, shape, dt=F32):
        return nc.dram_tensor(name, list(shape), dt, kind="ExternalOutput").ap()

    x_d = din("x", [NT * 128, D])
    sc_d = din("sconv", [2 * DEC_PER, CW])
    sr_d = din("sret", [DEC_PER, H, HD, HD])
    w_in_d = din("w_in", [D, INC])
    w_out_d = din("w_out", [D, D])
    w_gate_d = din("w_gate", [D, DFF])
    w_up_d = din("w_up", [D, DFF])
    w_down_d = din("w_down", [DFF, D])
    g1T_d = din("g1T", [128, 8])
    g2T_d = din("g2T", [128, 8])
    gngT_d = din("gngT", [128, 4])
    cwT_d = din("cwT", [128, 4, 3])
    gfb_d = din("gfb", [128, D])
    tabs_d = din("tabs", [NT, 128, 1536])
    maskp_d = din("mask_p", [128, 128])
    masks_d = din("mask_s", [128, 128])
    m1_d = din("m1", [128, DEC_PER])
    m2_d = din("m2", [128, DEC_PER, 128])
    ident_d = din("ident", [128, 128])

    y_d = dout("y", [NT * 128, D])
    convp_d = dout("convp", [2, CW])
    retp_d = dout("retp", [H, HD, HD])
    convs_d = dout("convs", [2 * DEC_PER, CW])
    rets_d = dout("rets", [DEC_PER, H, HD, HD])

    ws_d = nc.dram_tensor("wscratch", [NPAN_IN + NPAN_GU + 2, 128, PANW], BF16, kind="Internal").ap()

    with ExitStack() as st:
        ARENA_W = 52600
        arena = st.enter_context(nc.sbuf_tensor("arena", [128, ARENA_W], F32))
        cur = [0]

        def alloc(shape, dt=F32):
            n = int(np.prod(shape))
            words = n if dt == F32 else (n + 1) // 2
            words = (words + 7) // 8 * 8
            off = cur[0]
            cur[0] += words
            assert cur[0] <= ARENA_W, ("SBUF arena overflow", cur[0])
            ap = arena[:, off:off + words]
            if dt != F32:
                ap = ap.bitcast(dt)[:, 0:n]
            else:
                ap = ap[:, 0:n]
            if len(shape) == 2:
                ap = ap.rearrange("p (a b) -> p a b", a=shape[0])
            elif len(shape) == 3:
                ap = ap.rearrange("p (a b c) -> p a b c", a=shape[0], b=shape[1])
            return ap

        wpan = [alloc([PANW], BF16) for _ in range(4)]
        wdn = alloc([NFC, D], BF16)
        xres = [alloc([D]) for _ in range(4)]
        xin = [alloc([D]) for _ in range(2)]
        hb0 = alloc([D], BF16)
        hb3 = alloc([D], BF16)
        h1T = alloc([8, 512], BF16)
        h2T = alloc([8, 512], BF16)
        oT = alloc([8, 512], BF16)
        Tst = alloc([H, HD])
        Rbf = alloc([H, HD], BF16)
        junk = alloc([D], BF16)
        gfb = alloc([D])
        sgg = [alloc([512], BF16) for _ in range(2)]
        maskp = alloc([128])
        masks = alloc([128])
        identf = alloc([128])
        identb = alloc([128], BF16)
        g1T = alloc([8])
        g2T = alloc([8])
        gngT = alloc([4])
        cwT = alloc([4, 3])
        mhalf = alloc([8])
        ss = alloc([8])
        ms = alloc([8])
        rstd = alloc([8])
        bnst = alloc([H, 6])
        bnag = alloc([H, 2])
        gvar = alloc([H])
        grstd = alloc([H])
        gnmr = alloc([H])
        m1c = alloc([DEC_PER])
        convT = alloc([32])
        convo = alloc([128])
        sc_tok = alloc([128])
        qk_tok = alloc([4, 2, 512], BF16)
        v_tok = alloc([4, 512], BF16)
        ra = alloc([512])
        rm = alloc([512])
        tabh = [alloc([768]) for _ in range(2)]
        qkT = alloc([8, 128], BF16)
        ST_sb = alloc([H, 128], BF16)
        on_sb = alloc([512], BF16)
        sgT = alloc([4, 512], BF16)
        Ccp = alloc([512])
        ubuf1 = alloc([2 + 512])
        halo = alloc([4, 2 * 1])
        Rf = alloc([DEC_PER // 2, HD])
        qf = alloc([H, 128])
        Vblk = rm.bitcast(BF16).rearrange("p (b v) -> p b v", b=DEC_PER // 2)
        aT = alloc([NFC, 512], BF16)

        banks = [st.enter_context(nc.psum_tensor("pb%d" % i, [128, 512], F32)) for i in range(7)]
        ptb = st.enter_context(nc.psum_tensor("ptb", [128, 1024], BF16))
        DBK = [0, 1, 2, 3]
        BS, BO, BI = 4, 5, 6
        bankB = [Buf("bank%d" % i, excl=True) for i in range(7)]
        ptB = Buf("ptb", excl=True)
        db_i = [0]

        def next_bank():
            i = DBK[db_i[0] % len(DBK)]
            db_i[0] += 1
            return banks[i], bankB[i]

        B = lambda n: Buf(n)
        wpanB = [B("wpan%d" % i) for i in range(4)]
        wdnB = B("wdn")
        xresB = [B("xres%d" % i) for i in range(4)]
        xinB = [B("xin0"), B("xin1")]
        hb0B, hb3B = B("hb0"), B("hb3")
        h1TB = [B("h1T%d" % i) for i in range(4)]
        h2TB = [B("h2T%d" % i) for i in range(4)]
        oTrB = [B("oTr%d" % i) for i in range(4)]
        oTcB = B("oTc")
        TstB, RbfB = B("Tst"), B("Rbf")
        junkB = B("junk")
        constB = B("const")
        sggB = [B("sgg0"), B("sgg1")]
        ssB, msB, rstdB = B("ss"), B("ms"), B("rstd")
        bnB, gnB = B("bn"), B("gn")
        qkB = [[B("qk%d_%d" % (s, j)) for j in range(2)] for s in range(4)]
        vB = [B("v%d" % s) for s in range(4)]
        raB, rmB = B("ra"), B("rm")
        tabB = [B("tab0"), B("tab1")]
        qkTB, STB, onB = B("qkT"), B("ST"), B("on")
        sgTB = B("sgT")
        CcpB = B("Ccp")
        uB1 = B("u")
        haloB = [B("halo%d" % i) for i in range(4)]
        aTB = [B("aT%d" % i) for i in range(NFC)]
        RfB, qfB = B("Rf"), B("qf")
        VblkB = rmB
        gvB, grB, gmB = B("gvar"), B("grstd"), B("gnmr")
        convTB, convoB, scB = B("convT"), B("convo"), B("sc")
        wsB = [B("ws%d" % i) for i in range(NPAN_IN + NPAN_GU + 2)]

        pe, act, dve, pool, sp = P.pe, P.act, P.dve, P.pool, P.sp

        d_const = P.dsem("const")
        d_ws = [P.dsem("ws%d" % i) for i in range(NPAN_IN + NPAN_GU + 2)]
        d_wpan = [P.dsem("wpan%d" % i) for i in range(4)]
        d_wdn = P.dsem("wdn")
        d_x = [P.dsem("x%d" % i) for i in range(4)]
        d_xin = [P.dsem("xin0"), P.dsem("xin1")]
        d_y = [P.dsem("y%d" % i) for i in range(4)]
        xin_i = [0]
        d_tab = [P.dsem("tab0"), P.dsem("tab1")]
        tab_i = [0]
        d_misc = P.dsem("misc")
        d_cv, d_rp = P.dsem("cv"), P.dsem("rp")
        d_rf, d_rs = P.dsem("rf"), P.dsem("rs")
        d_ys, d_cvs = P.dsem("ys"), P.dsem("cvs")

        def cload(dst, src):
            P.dma(sp, lambda e: e.dma_start(out=dst, in_=src), d_const, writes=[constB])

        cload(identf, ident_d[:, :])
        cload(maskp, maskp_d[:, :])
        cload(masks, masks_d[:, :])
        cload(g1T, g1T_d[:, :])
        cload(g2T, g2T_d[:, :])
        cload(gngT, gngT_d[:, :])
        cload(cwT, cwT_d[:, :, :])
        cload(gfb, gfb_d[:, :])
        cload(m1c, m1_d[:, :])
        identbB, mhB = B("identb"), B("mhalf")
        P.op(dve, lambda e: e.tensor_copy(out=identb, in_=identf), reads=[constB], writes=[identbB])
        P.op(pool, lambda e: e.memset(mhalf, -0.5), writes=[mhB])

        def cast_in_panel(j):
            dst = ws_d[j]
            if j < 4:
                P.dma(pool, lambda e: e.dma_start(
                    out=dst.rearrange("p (kc n) -> p kc n", kc=8),
                    in_=w_in_d[:, j * 512:(j + 1) * 512].rearrange("(kc p) n -> p kc n", p=128)),
                    d_ws[j], writes=[wsB[j]])
            else:
                cc = j - 4
                dv = dst[:, 0:8 * 384].rearrange("p (kc n) -> p kc n", kc=8)
                c0 = 2048 + 384 * cc
                P.dma(pool, lambda e: e.dma_start(
                    out=dv, in_=w_in_d[:, c0:c0 + 384].rearrange("(kc p) n -> p kc n", p=128)),
                    d_ws[j], writes=[wsB[j]])

        def cast_gu_panel(j):
            dst = ws_d[NPAN_IN + j].rearrange("p (kc t n) -> p kc t n", kc=8, t=2)
            for t, wd in enumerate((w_gate_d, w_up_d)):
                P.dma(pool, lambda e, t=t, wd=wd: e.dma_start(
                    out=dst[:, :, t, :],
                    in_=wd[:, j * 256:(j + 1) * 256].rearrange("(kc p) n -> p kc n", p=128)),
                    d_ws[NPAN_IN + j], writes=[wsB[NPAN_IN + j]])

        pool.wait(d_const, d_const.count)
        for j in range(NPAN_IN):
            cast_in_panel(j)
        for hh in range(2):
            jo = NPAN_IN + NPAN_GU + hh
            P.dma(pool, lambda e: e.dma_start(
                out=ws_d[jo].rearrange("p (kc n) -> p kc n", kc=8),
                in_=w_out_d[:, hh * 512:(hh + 1) * 512].rearrange("(kc p) n -> p kc n", p=128)),
                d_ws[jo], writes=[wsB[jo]])
        for j in range(NPAN_GU):
            cast_gu_panel(j)
        P.dma(pool, lambda e: e.dma_start(out=wdn, in_=w_down_d.rearrange("(fc p) n -> p fc n", p=128)),
              d_wdn, writes=[wdnB])
        pending_casts = []

        def issue_casts(n):
            for _ in range(n):
                if pending_casts:
                    pending_casts.pop(0)()

        pan_i = {"A": 0, "B": 0}

        def load_panel(idx, width=PANW, pool_="B"):
            k = pan_i[pool_] % 2 + (0 if pool_ == "A" else 2)
            pan_i[pool_] += 1
            P.dma(sp, lambda e: e.dma_start(out=wpan[k][:, 0:width], in_=ws_d[idx][:, 0:width]), d_wpan[k],
                  reads=[wsB[idx]], writes=[wpanB[k]])
            return wpan[k], wpanB[k]

        class PanelStream:
            def __init__(self, specs, pool_="B", ahead=True):
                self.specs, self.pool_, self.got, self.ahead = specs, pool_, {}, ahead

            def _ld(self, i):
                if i < len(self.specs) and i not in self.got:
                    idx, width = self.specs[i]
                    self.got[i] = load_panel(idx, width, self.pool_)

            def get(self, i):
                self._ld(i)
                if self.ahead:
                    self._ld(i + 1)
                return self.got[i]

        I32 = mybir.dt.int32
        nt1 = alloc([8])
        ntB = B("nt1")
        use_dve = [True]

        def rsq(x_ap, out_ap, xB, outB):
            if not use_dve[0]:
                k = x_ap.shape[1]
                P.op(pool, lambda e: e.tensor_tensor(out=out_ap, in0=x_ap, in1=mhalf[:, 0:k], op=ALU.pow),
                     reads=[xB, mhB], writes=[outB])
                return
            k = x_ap.shape[1]
            t1 = nt1[:, 0:k]
            P.op(dve, lambda e: e.tensor_scalar(out=out_ap.bitcast(I32), in0=x_ap.bitcast(I32), scalar1=-0.5,
                                                scalar2=float(0x5f3759df), op0=ALU.mult, op1=ALU.add),
                 reads=[xB], writes=[outB])
            for _ in range(2):
                if k == 1:
                    P.op(dve, lambda e: e.scalar_tensor_tensor(out=t1, in0=out_ap, scalar=x_ap[:, 0:1], in1=out_ap,
                                                               op0=ALU.mult, op1=ALU.mult),
                         reads=[outB, xB], writes=[ntB])
                else:
                    P.op(dve, lambda e: e.tensor_tensor(out=t1, in0=out_ap, in1=out_ap, op=ALU.mult),
                         reads=[outB], writes=[ntB])
                    P.op(dve, lambda e: e.tensor_tensor(out=t1, in0=t1, in1=x_ap, op=ALU.mult),
                         reads=[ntB, xB], writes=[ntB])
                P.op(dve, lambda e: e.tensor_scalar(out=t1, in0=t1, scalar1=-0.5, scalar2=1.5, op0=ALU.mult,
                                                    op1=ALU.add), reads=[ntB], writes=[ntB])
                P.op(dve, lambda e: e.tensor_tensor(out=out_ap, in0=out_ap, in1=t1, op=ALU.mult),
                     reads=[outB, ntB], writes=[outB])

        def rms_sq(src_ap, srcB, col):
            P.op(act, lambda e: e.activation(out=junk, in_=src_ap, func=AF.Square, scale=1.0 / 32.0,
                                             accum_out=ss[:, col:col + 1]),
                 reads=[srcB], writes=[ssB])

        def rms_rs(col):
            P.op(dve, lambda e: e.tensor_scalar_add(out=ms[:, col:col + 1], in0=ss[:, col:col + 1], scalar1=EPS),
                 reads=[ssB], writes=[msB])
            rsq(ms[:, col:col + 1], rstd[:, col:col + 1], msB, rstdB)

        def to_featmajor(s, gT, hT, hTB, hb, hbB):
            ptv = ptb[:, :].rearrange("p (a b) -> p a b", a=8)
            P.group(pe, [lambda e, kc=kc: e.transpose(out=ptv[:, kc, :], in_=hb[:, kc * 128:(kc + 1) * 128],
                                                     identity=identb) for kc in range(8)],
                    reads=[hbB, identbB], writes=[ptB])
            P.op(dve, lambda e: e.tensor_tensor(out=hT[:, :, s * 128:(s + 1) * 128], in0=ptv,
                                                in1=gT.unsqueeze(2).to_broadcast([128, 8, 128]), op=ALU.mult),
                 reads=[ptB, constB], writes=[hTB[s]])

        if groups is None:
            groups = [[0, 1, 2, 3], [4, 5, 6, 7], [8, 9, 10, 11], [12, 13, 14, 15], [16]]

        class Ctx:
            pass

        ctxs = []
        for gi, tiles in enumerate(groups):
            c = Ctx()
            c.gi, c.tiles, c.T, c.N = gi, tiles, len(tiles), 128 * len(tiles)
            c.sample = (tiles[0] == 16)
            c.nseq, c.L = (DEC_PER, DEC_SEQ) if c.sample else (1, c.N)
            c.gC = gC_s if c.sample else gC_p
            c.mask = masks if c.sample else maskp
            c.xs = list(range(c.T))
            ctxs.append(c)

        ptv = ptb[:, :].rearrange("p (a b) -> p a b", a=8)

        def phase0(c, s, part):
            gt = c.tiles[s]
            if part == "load":
                k = xin_i[0] % 2
                xin_i[0] += 1
                c.xk[s] = k
                P.dma(sp, lambda e: e.dma_start(out=xin[k], in_=x_d[gt * 128:(gt + 1) * 128, :]),
                      d_xin[k], writes=[xinB[k]])
            elif part == "sq":
                k = c.xk[s]
                rms_sq(xin[k], xinB[k], 0)
            elif part == 0:
                k = c.xk[s]
                rms_rs(0)
                P.op(dve, lambda e: e.tensor_scalar(out=hb0, in0=xin[k], scalar1=rstd[:, 0:1], scalar2=None,
                                                    op0=ALU.mult),
                     reads=[xinB[k], rstdB], writes=[hb0B])
            else:
                to_featmajor(s, g1T, h1T, h1TB, hb0, hb0B)

        def gen_phase1a(c):
            c.qstream = PanelStream([(0, PANW), (1, PANW), (2, PANW), (3, PANW)], ahead=c.sample)
            for blk in range(3):
                pan, panB = c.qstream.get(blk)
                panv = pan.rearrange("p (kc n) -> p kc n", kc=8)
                for s, gt in enumerate(c.tiles):
                    bk, bkB = next_bank()
                    P.group(pe, [lambda e, kc=kc: e.matmul(
                        bk[:, :], lhsT=h1T[:, kc, s * 128:(s + 1) * 128], rhs=panv[:, kc, :],
                        start=(kc == 0), stop=(kc == 7)) for kc in range(8)],
                        reads=[h1TB[s], panB], writes=[bkB])
                    if blk == 2:
                        P.op(act, lambda e: e.activation(out=v_tok[:, s, :], in_=bk[:, :], func=AF.Copy),
                             reads=[bkB], writes=[vB[s]])
                        yield (1.8, 0.0)
                        continue
                    c_off = 0 if blk == 0 else 768
                    tk_ = tab_i[0] % 2
                    tab_i[0] += 1
                    tab, tabB_ = tabh[tk_], tabB[tk_]
                    P.dma(sp, lambda e: e.dma_start(out=tab, in_=tabs_d[gt][:, c_off:c_off + 768]),
                          d_tab[tk_], writes=[tabB_])
                    Ct = tab[:, 0:256].rearrange("p (h d) -> p h d", h=4)
                    St = tab[:, 256:768].rearrange("p (h t d) -> p h t d", h=4, t=2)
                    ps4 = bk[:, :].rearrange("p (h t d) -> p h t d", h=4, t=2)
                    ra4 = ra.rearrange("p (h t d) -> p h t d", h=4, t=2)
                    rm4 = rm.rearrange("p (h t d) -> p h t d", h=4, t=2)
                    P.op(dve, lambda e: e.tensor_tensor(
                        out=ra4, in0=ps4, in1=Ct.unsqueeze(2).to_broadcast([128, 4, 2, 64]), op=ALU.mult),
                        reads=[bkB, tabB_], writes=[raB])
                    P.op(dve, lambda e: e.tensor_tensor(
                        out=rm4[:, :, 0, :], in0=ps4[:, :, 1, :], in1=St[:, :, 0, :], op=ALU.mult),
                        reads=[bkB, tabB_], writes=[rmB])
                    P.op(dve, lambda e: e.tensor_tensor(
                        out=rm4[:, :, 1, :], in0=ps4[:, :, 0, :], in1=St[:, :, 1, :], op=ALU.mult),
                        reads=[bkB, tabB_], writes=[rmB])
                    P.op(dve if use_dve[0] else pool,
                         lambda e: e.tensor_tensor(out=qk_tok[:, s, blk, :], in0=ra, in1=rm, op=ALU.add),
                         reads=[raB, rmB], writes=[qkB[s][blk]])
                    yield (1.8, 3.4)
                if c.gi == 0:
                    issue_casts(1)

        def gen_gpanel(c):
            N, T = c.N, c.T
            pan, panB = c.qstream.get(3)
            panv = pan.rearrange("p (kc n) -> p kc n", kc=8)
            for cc in range(4):
                bk, bkB = next_bank()
                P.group(pe, [lambda e, kc=kc: e.matmul(
                    bk[:, 0:N], lhsT=panv[:, kc, cc * 128:(cc + 1) * 128], rhs=h1T[:, kc, 0:N],
                    start=(kc == 0), stop=(kc == 7)) for kc in range(8)],
                    reads=h1TB[:T] + [panB], writes=[bkB])
                P.op(act, lambda e: e.activation(out=sgT[:, cc, 0:N], in_=bk[:, 0:N], func=AF.Silu),
                     reads=[bkB], writes=[sgTB])
                yield (1.8, 0.0)
            if c.gi == 0:
                issue_casts(1)

        def gen_conv(c):
            N, T, nseq, L, sample = c.N, c.T, c.nseq, c.L, c.sample
            v3 = lambda ap: ap.rearrange("p (b l) -> p b l", b=nseq)
            if sample:
                P.dma(sp, lambda e: e.dma_start(out=sc_tok[0:32, :], in_=sc_d[:, 0:128]), d_misc, writes=[scB])
            cstream = PanelStream([(4 + i, 8 * 384) for i in range(4)], ahead=sample)
            for cc in range(4):
                pan, panB = cstream.get(cc)
                panv = pan[:, 0:8 * 384].rearrange("p (kc n) -> p kc n", kc=8)
                u = ubuf1
                u3 = u[:, 0:nseq * (2 + L)].rearrange("p (b l) -> p b l", b=nseq)
                if c.tiles[0] == 0:
                    P.op(dve, lambda e: e.memset(u[:, 0:2], 0.0), writes=[uB1])
                elif not sample:
                    P.op(dve, lambda e: e.tensor_copy(out=u[:, 0:2], in_=halo[:, cc, :]),
                         reads=[haloB[cc]], writes=[uB1])
                if sample:
                    bk, bkB = next_bank()
                    P.op(pe, lambda e: e.matmul(
                        bk[:, 0:32], lhsT=sc_tok[0:32, :], rhs=identf[0:32, 0:32],
                        start=True, stop=True), reads=[scB, constB], writes=[bkB])
                    if cc + 1 < 4:
                        P.dma(sp, lambda e: e.dma_start(out=sc_tok[0:32, :],
                                                        in_=sc_d[:, (cc + 1) * 128:(cc + 2) * 128]),
                              d_misc, writes=[scB])
                    P.op(dve, lambda e: e.tensor_copy(
                        out=u3[:, :, 0:2], in_=bk[:, 0:32].rearrange("p (b j) -> p b j", b=DEC_PER)),
                        reads=[bkB], writes=[uB1])
                bks = {}
                for k in (1, 2, 0):
                    bk, bkB_ = next_bank()
                    bks[k] = (bk, bkB_)
                    P.group(pe, [lambda e, kc=kc: e.matmul(
                        bk[:, 0:N], lhsT=panv[:, kc, k * 128:(k + 1) * 128], rhs=h1T[:, kc, 0:N],
                        start=(kc == 0), stop=(kc == 7)) for kc in range(8)],
                        reads=h1TB[:T] + [panB], writes=[bkB_])
                    if k == 1:
                        bkC, bkCB = bk, bkB_
                        P.op(act, lambda e: e.activation(out=Ccp[:, 0:N], in_=bkC[:, 0:N], func=AF.Copy),
                             reads=[bkCB], writes=[CcpB])
                    elif k == 2:
                        bkX, bkXB = bk, bkB_
                        P.op(dve, lambda e: e.tensor_tensor(
                            out=u3[:, :, 2:2 + L], in0=v3(bkX[:, 0:N]), in1=v3(Ccp[:, 0:N]), op=ALU.mult),
                            reads=[bkXB, CcpB], writes=[uB1])
                        ca3 = v3(Ccp[:, 0:N])
                        P.op(dve, lambda e: e.tensor_scalar(
                            out=ca3, in0=u3[:, :, 2:2 + L], scalar1=cwT[:, cc, 2:3], scalar2=None, op0=ALU.mult),
                            reads=[uB1, constB], writes=[CcpB])
                        for jj in (1, 0):
                            P.op(dve, lambda e: e.scalar_tensor_tensor(
                                out=ca3, in0=u3[:, :, jj:jj + L], scalar=cwT[:, cc, jj:jj + 1], in1=ca3,
                                op0=ALU.mult, op1=ALU.add),
                                reads=[uB1, CcpB, constB], writes=[CcpB])
                    else:
                        bkG, bkGB = bk, bkB_
                        P.op(dve, lambda e: e.tensor_tensor(
                            out=oT[:, 4 + cc, 0:N], in0=bkG[:, 0:N], in1=Ccp[:, 0:N], op=ALU.mult),
                            reads=[bkGB, CcpB], writes=[oTcB])
                    yield (1.8, 2.0)
                if c.tiles[-1] == 15 or sample:
                    nr = 2 * nseq
                    P.op(dve, lambda e: e.tensor_copy(
                        out=convT[:, 0:nr].rearrange("p (b j) -> p b j", b=nseq), in_=u3[:, :, L:L + 2]),
                        reads=[uB1], writes=[convTB])
                    yield (0.0, 4.0)
                    bk, bkB = next_bank()
                    P.op(pe, lambda e: e.matmul(bk[0:nr, 0:128], lhsT=convT[:, 0:nr], rhs=identf,
                                                start=True, stop=True),
                         reads=[convTB, constB], writes=[bkB])
                    P.op(dve, lambda e: e.tensor_copy(out=convo[0:nr, :], in_=bk[0:nr, 0:128]),
                         reads=[bkB], writes=[convoB])
                    dst = convs_d if sample else convp_d
                    P.dma(sp if sample else pool,
                          lambda e: e.dma_start(out=dst[:, cc * 128:(cc + 1) * 128], in_=convo[0:nr, :]),
                          d_cvs if sample else d_cv, reads=[convoB])
                else:
                    P.op(dve, lambda e: e.tensor_copy(out=halo[:, cc, :], in_=u[:, N:N + 2]),
                         reads=[uB1], writes=[haloB[cc]])
                if c.gi == 0:
                    issue_casts(1)

        def gen_ret(c, s):
            gt = c.tiles[s]
            sample, gC, mask = c.sample, c.gC, c.mask
            first = (gt == 0)
            deferred_act = None
            P.group(pe, [lambda e, a=a: e.transpose(
                out=ptv[:, a, :], in_=qk_tok[:, s, a // 4, (a % 4) * 128:(a % 4 + 1) * 128], identity=identb)
                for a in range(8)],
                reads=[qkB[s][0], qkB[s][1], identbB], writes=[ptB])
            P.op(dve, lambda e: e.tensor_copy(out=qkT, in_=ptv), reads=[ptB], writes=[qkTB])
            yield (0.9, 2.5)
            Sv = banks[BS][:, :].rearrange("p (h i) -> p h i", h=4)
            P.group(pe, [lambda e, h=h: e.matmul(Sv[:, h, :], lhsT=qkT[:, 4 + h, :], rhs=qkT[:, h, :],
                                                 start=True, stop=True) for h in range(4)],
                    reads=[qkTB], writes=[bankB[BS]])
            P.op(dve, lambda e: e.tensor_tensor(
                out=ST_sb, in0=Sv, in1=mask.unsqueeze(1).to_broadcast([128, 4, 128]), op=ALU.mult),
                reads=[bankB[BS], constB], writes=[STB])
            yield (0.5, 2.5)
            Ov = banks[BO][:, :].rearrange("p (h v) -> p h v", h=4)
            Iv = banks[BI][:, :].rearrange("p (h v) -> p h v", h=4)
            if not sample:
                fns = []
                for h in range(4):
                    fns.append(lambda e, h=h: e.matmul(
                        Ov[:, h, :], lhsT=ST_sb[:, h, :], rhs=v_tok[:, s, h * 128:(h + 1) * 128],
                        start=True, stop=first))
                    if not first:
                        fns.append(lambda e, h=h: e.matmul(Ov[:, h, :], lhsT=qkT[:, h, :], rhs=Rbf[:, h, :],
                                                           start=False, stop=True))
                P.group(pe, fns, reads=[STB, vB[s], qkTB] + ([] if first else [RbfB]), writes=[bankB[BO]])
                P.group(pe, [lambda e, h=h: e.matmul(
                    Iv[:, h, :], lhsT=qk_tok[:, s, 1, h * 128:(h + 1) * 128],
                    rhs=v_tok[:, s, h * 128:(h + 1) * 128], start=True, stop=True) for h in range(4)],
                    reads=[qkB[s][1], vB[s]], writes=[bankB[BI]])
                yield (1.3, 1.2)
                if first:
                    P.op(dve, lambda e: e.tensor_copy(out=Tst, in_=Iv), reads=[bankB[BI]], writes=[TstB])
                else:
                    for h in range(4):
                        P.op(dve, lambda e, h=h: e.scalar_tensor_tensor(
                            out=Tst[:, h, :], in0=Tst[:, h, :], scalar=float(gC[h]), in1=Iv[:, h, :],
                            op0=ALU.mult, op1=ALU.add),
                            reads=[bankB[BI], TstB], writes=[TstB])
                if gt < 15:
                    def rbf_ops():
                        for h in range(4):
                            P.op(act, lambda e, h=h: e.activation(out=Rbf[:, h, :], in_=Tst[:, h, :], func=AF.Copy,
                                                                  scale=float(gC[h])),
                                 reads=[TstB], writes=[RbfB])
                    deferred_act = rbf_ops
                else:
                    for h in range(4):
                        P.op(act, lambda e, h=h: e.activation(out=Tst[:, h, :], in_=Tst[:, h, :], func=AF.Copy,
                                                              scale=float(gC[h])),
                             reads=[TstB], writes=[TstB])
                    P.dma(pool, lambda e: e.dma_start(out=retp_d.rearrange("h d v -> d h v"), in_=Tst),
                          d_rp, reads=[TstB])
            else:
                XTv = banks[BS][:, :].rearrange("p (h i) -> p h i", h=4)
                HB = DEC_PER // 2
                P.op(act, lambda e: e.activation(out=qf, in_=qkT[:, 0:4, :], func=AF.Copy),
                     reads=[qkTB], writes=[qfB])
                Vb = [(Vblk, VblkB), (ra.bitcast(BF16).rearrange("p (b v) -> p b v", b=HB), raB)]
                its = [(h, hf) for h in range(4) for hf in range(2)]

                def mk_vblk(i):
                    h_, hf_ = its[i]
                    vb, vbB = Vb[i % 2]
                    P.op(dve if use_dve[0] else pool, lambda e: e.tensor_tensor(
                        out=vb, in0=v_tok[:, s, h_ * 128:(h_ + 1) * 128].unsqueeze(1).to_broadcast([128, HB, 128]),
                        in1=m1c[:, hf_ * HB:(hf_ + 1) * HB].unsqueeze(2).to_broadcast([128, HB, 128]), op=ALU.mult),
                        reads=[vB[s], constB], writes=[vbB])

                def ld_rf(i):
                    h_, hf_ = its[i]
                    P.dma(pool, lambda e: e.dma_start(
                        out=Rf, in_=sr_d[hf_ * HB:(hf_ + 1) * HB, h_, :, :].rearrange("b d v -> d b v")),
                        d_rf, writes=[RfB])

                mk_vblk(0)
                ld_rf(0)
                for i, (h, hf) in enumerate(its):
                    b0 = hf * HB
                    vb, vbB = Vb[i % 2]
                    P.group(pe, [lambda e, bl=bl: e.matmul(
                        XTv[:, h, (b0 + bl) * DEC_SEQ:(b0 + bl + 1) * DEC_SEQ], lhsT=Rf[:, bl, :],
                        rhs=qf[:, h, (b0 + bl) * DEC_SEQ:(b0 + bl + 1) * DEC_SEQ], start=True, stop=True)
                        for bl in range(HB)],
                        reads=[RfB, qfB, STB], writes=[bankB[BS]])
                    bk2 = [next_bank(), next_bank()]
                    for q4 in range(2):
                        bk, bkB = bk2[q4]
                        P.op(pe, lambda e: e.matmul(
                            bk[:, :], lhsT=qk_tok[:, s, 1, h * 128:(h + 1) * 128],
                            rhs=vb[:, 4 * q4:4 * q4 + 4, :], start=True, stop=True),
                            reads=[qkB[s][1], vbB], writes=[bkB])
                    if i + 1 < len(its):
                        mk_vblk(i + 1)
                    P.op(dve, lambda e: e.tensor_scalar(out=Rf, in0=Rf, scalar1=float(gC[h]), scalar2=None,
                                                        op0=ALU.mult), reads=[RfB], writes=[RfB])
                    for q4 in range(2):
                        bk, bkB = bk2[q4]
                        P.op(dve, lambda e: e.scalar_tensor_tensor(
                            out=Rf[:, 4 * q4:4 * q4 + 4, :], in0=bk[:, :].rearrange("p (b v) -> p b v", b=4),
                            scalar=float(gC[h]), in1=Rf[:, 4 * q4:4 * q4 + 4, :], op0=ALU.mult, op1=ALU.add),
                            reads=[bkB, RfB], writes=[RfB])
                    P.dma(pool, lambda e: e.dma_start(
                        out=rets_d[b0:b0 + HB, h, :, :].rearrange("b d v -> d b v"), in_=Rf), d_rs, reads=[RfB])
                    if i + 1 < len(its):
                        ld_rf(i + 1)
                    yield (1.0, 8.0)
                P.op(act, lambda e: e.activation(out=ra, in_=banks[BS][:, :], func=AF.Copy),
                     reads=[bankB[BS]], writes=[raB])
                fns = []
                for h in range(4):
                    fns.append(lambda e, h=h: e.matmul(
                        Ov[:, h, :], lhsT=ST_sb[:, h, :], rhs=v_tok[:, s, h * 128:(h + 1) * 128],
                        start=True, stop=False))
                    fns.append(lambda e, h=h: e.matmul(
                        Ov[:, h, :], lhsT=ra[:, h * 128:(h + 1) * 128], rhs=identf, start=False, stop=True))
                P.group(pe, fns, reads=[STB, vB[s], raB, constB], writes=[bankB[BO]])
            pass
            for h in range(4):
                P.op(dve, lambda e, h=h: e.bn_stats(out=bnst[:, h, :], in_=Ov[:, h, :]),
                     reads=[bankB[BO]], writes=[bnB])
            for h in range(4):
                P.op(dve, lambda e, h=h: e.bn_aggr(out=bnag[:, h, :], in_=bnst[:, h, :]),
                     reads=[bnB], writes=[gnB])
            P.op(dve, lambda e: e.tensor_scalar_add(out=gvar, in0=bnag[:, :, 1], scalar1=GN_EPS),
                 reads=[gnB], writes=[gvB])
            yield (0.0 if not sample else 1.3, 3.5)
            rsq(gvar, grstd, gvB, grB)
            P.op(dve, lambda e: e.scalar_tensor_tensor(out=gnmr, in0=bnag[:, :, 0], scalar=-1.0, in1=grstd,
                                                       op0=ALU.mult, op1=ALU.mult),
                 reads=[gnB, grB], writes=[gmB])
            yield (0.0, 2.5)
            if deferred_act is not None:
                deferred_act()
            for h in range(4):
                P.op(act, lambda e, h=h: e.activation(
                    out=on_sb[:, h * 128:(h + 1) * 128], in_=Ov[:, h, :], func=AF.Identity,
                    scale=grstd[:, h:h + 1], bias=gnmr[:, h:h + 1]),
                    reads=[bankB[BO], grB, gmB], writes=[onB])
            yield (0.0, 4.0)
            P.group(pe, [lambda e, h=h: e.transpose(out=ptv[:, h, :], in_=on_sb[:, h * 128:(h + 1) * 128],
                                                    identity=identb) for h in range(4)],
                    reads=[onB, identbB], writes=[ptB])
            for h in range(4):
                P.op(dve, lambda e, h=h: e.scalar_tensor_tensor(
                    out=oT[:, h, s * 128:(s + 1) * 128], in0=ptv[:, h, :], scalar=gngT[:, h:h + 1],
                    in1=sgT[:, h, s * 128:(s + 1) * 128], op0=ALU.mult, op1=ALU.mult),
                    reads=[ptB, sgTB, constB], writes=[oTrB[s]])
            yield (0.5, 0.5)

        def gen_phase3(c, s, wo):
            xs = c.xs[s]
            gt = c.tiles[s]
            P.dma(sp, lambda e: e.dma_start(out=xres[xs], in_=x_d[gt * 128:(gt + 1) * 128, :]),
                  d_x[xs], writes=[xresB[xs]])
            for half in range(2):
                pan, panB = wo[half]
                panv = pan.rearrange("p (kc n) -> p kc n", kc=8)
                bk, bkB = next_bank()
                P.group(pe, [lambda e, kc=kc: e.matmul(
                    bk[:, :], lhsT=oT[:, kc, s * 128:(s + 1) * 128], rhs=panv[:, kc, :],
                    start=(kc == 0), stop=(kc == 7)) for kc in range(8)],
                    reads=[oTrB[s], oTcB, panB], writes=[bkB])
                P.op(dve, lambda e: e.tensor_tensor(
                    out=xres[xs][:, half * 512:(half + 1) * 512], in0=bk[:, :],
                    in1=xres[xs][:, half * 512:(half + 1) * 512], op=ALU.add),
                    reads=[bkB, xresB[xs]], writes=[xresB[xs]])
                yield (1.8, 0.0) if half == 0 else (1.8, 2.5)
            rms_sq(xres[xs], xresB[xs], 1)
            yield (0.0, 1.5)
            rms_rs(1)
            P.op(dve, lambda e: e.tensor_scalar(out=hb3, in0=xres[xs], scalar1=rstd[:, 1:2], scalar2=None,
                                                op0=ALU.mult),
                 reads=[xresB[xs], rstdB], writes=[hb3B])
            yield (0.0, 4.5)
            to_featmajor(s, g2T, h2T, h2TB, hb3, hb3B)
            yield (0.9, 1.2)

        def load_wout(pool_="B"):
            return [load_panel(NPAN_IN + NPAN_GU + hh, PANW, pool_) for hh in range(2)]

        def gen_phase4(c):
            N, T = c.N, c.T
            if c.sample:
                last_group = (c is ctxs[-1])
                for j in range(NPAN_GU):
                    pan, panB = load_panel(NPAN_IN + j, PANW, "B" if (last_group and (j // 2) % 2 == 1) else "A")
                    panv = pan.rearrange("p (kc t n) -> p kc t n", kc=8, t=2)
                    k2 = j % 2
                    bks = []
                    for t in range(2):
                        bk, bkB_ = next_bank()
                        bks.append((bk, bkB_))
                        P.group(pe, [lambda e, kc=kc: e.matmul(
                            bk[:, 0:256], lhsT=h2T[:, kc, 0:128], rhs=panv[:, kc, t, :],
                            start=(kc == 0), stop=(kc == 7)) for kc in range(8)],
                            reads=[h2TB[0], panB], writes=[bkB_])
                    (bkg, bkgB), (bku, bkuB) = bks
                    P.op(act, lambda e: e.activation(out=sgg[k2][:, 0:256], in_=bkg[:, 0:256], func=AF.Silu),
                         reads=[bkgB], writes=[sggB[k2]])
                    P.op(dve, lambda e: e.tensor_tensor(out=sgg[k2][:, 0:256], in0=bku[:, 0:256],
                                                        in1=sgg[k2][:, 0:256], op=ALU.mult),
                         reads=[bkuB, sggB[k2]], writes=[sggB[k2]])
                    yield (1.9, 2.5)
                    P.group(pe, [lambda e, fl=fl: e.transpose(
                        out=ptv[:, fl, :], in_=sgg[k2][:, fl * 128:(fl + 1) * 128], identity=identb)
                        for fl in range(2)], reads=[sggB[k2], identbB], writes=[ptB])
                    P.op(dve, lambda e: e.tensor_copy(out=aT[:, 2 * j:2 * j + 2, 0:128], in_=ptv[:, 0:2, :]),
                         reads=[ptB], writes=[aTB[2 * j], aTB[2 * j + 1]])
                    yield (0.3, 1.2)
                return
            for j in range(NPAN_GU):
                pan, panB = load_panel(NPAN_IN + j, PANW, "A")
                panv = pan.rearrange("p (kc t n) -> p kc t n", kc=8, t=2)
                for fl in range(2):
                    fc = 2 * j + fl
                    k2 = fc % 2
                    for t in range(2):
                        bk, bkB_ = next_bank()
                        P.group(pe, [lambda e, kc=kc: e.matmul(
                            bk[:, 0:N], lhsT=panv[:, kc, t, fl * 128:(fl + 1) * 128], rhs=h2T[:, kc, 0:N],
                            start=(kc == 0), stop=(kc == 7)) for kc in range(8)],
                            reads=h2TB[:T] + [panB], writes=[bkB_])
                        if t == 0:
                            P.op(act, lambda e: e.activation(out=sgg[k2][:, 0:N], in_=bk[:, 0:N], func=AF.Silu),
                                 reads=[bkB_], writes=[sggB[k2]])
                        else:
                            P.op(dve, lambda e: e.tensor_tensor(
                                out=aT[:, fc, 0:N], in0=bk[:, 0:N], in1=sgg[k2][:, 0:N], op=ALU.mult),
                                reads=[bkB_, sggB[k2]], writes=[aTB[fc]])
                        yield (0.45 * T, 0.0)

        def gen_phase5(c):
            for s, gt in enumerate(c.tiles):
                xs = c.xs[s]
                for half in range(2):
                    bk, bkB = next_bank()
                    P.group(pe, [lambda e, fc=fc: e.matmul(
                        bk[:, :], lhsT=aT[:, fc, s * 128:(s + 1) * 128], rhs=wdn[:, fc, half * 512:(half + 1) * 512],
                        start=(fc == 0), stop=(fc == NFC - 1)) for fc in range(NFC)],
                        reads=aTB + [wdnB], writes=[bkB])
                    P.op(dve, lambda e: e.tensor_tensor(
                        out=xres[xs][:, half * 512:(half + 1) * 512], in0=bk[:, :],
                        in1=xres[xs][:, half * 512:(half + 1) * 512], op=ALU.add),
                        reads=[bkB, xresB[xs]], writes=[xresB[xs]])
                    if half == 1:
                        yield (4.9, 1.5)
                        rms_sq(xres[xs], xresB[xs], 2)
                        yield (0.0, 1.5)
                        rms_rs(2)
                        P.op(dve, lambda e: e.scalar_tensor_tensor(
                            out=xres[xs], in0=xres[xs], scalar=rstd[:, 2:3], in1=gfb, op0=ALU.mult, op1=ALU.mult),
                            reads=[xresB[xs], rstdB, constB], writes=[xresB[xs]])
                        P.dma(sp if c.sample else pool,
                              lambda e: e.dma_start(out=y_d[gt * 128:(gt + 1) * 128, :], in_=xres[xs]),
                              d_ys if c.sample else d_y[xs], reads=[xresB[xs]])
                        c.a5_done = s + 1
                        yield (0.0, 0.0)
                    else:
                        yield (4.9, 0.0)

        def step(g):
            try:
                next(g)
                return True
            except StopIteration:
                return False

        def gen_P0(c):
            c.xk = {}
            for s_ in range(c.T):
                phase0(c, s_, "load")
                yield (0.0, 3.5)
                phase0(c, s_, "sq")
                yield (0.0, 1.5)
                phase0(c, s_, 0)
                yield (0.0, 4.0)
                phase0(c, s_, 1)
                yield (0.9, 1.2)

        def gen_Q(c):
            for u in gen_phase1a(c):
                yield u
            for u in gen_gpanel(c):
                yield u

        def gen_R(c):
            for s_ in range(c.T):
                for u in gen_ret(c, s_):
                    yield u

        def gen_B3(c):
            wo = load_wout("A" if c.gi == 0 else "B")
            prev = c.prev
            for s_ in range(c.T):
                while prev is not None and s_ < prev.T and getattr(prev, "a5_done", 0) <= s_:
                    yield None
                for u in gen_phase3(c, s_, wo):
                    yield u

        class Task:
            def __init__(self, name, gen, deps, prio):
                self.name, self.gen, self.deps, self.prio = name, gen, deps, prio
                self.ready = 0.0
                self.done = False
                self.started = False

        tasks = []
        by = {}

        def add(name, genf, deps, prio, c=None):
            t = Task(name, genf, [d for d in deps if d is not None], prio)
            t.c = c
            tasks.append(t)
            by[name] = t
            return t

        ng = len(ctxs)
        for gi, c in enumerate(ctxs):
            c.prev = ctxs[gi - 1] if gi > 0 else None
            c.a5_done = 0
        g_ = lambda n, i: by.get("%s%d" % (n, i))
        for gi, c in enumerate(ctxs):
            add("P0%d" % gi, (lambda c=c: gen_P0(c)), [g_("Q", gi - 1), g_("C", gi - 1)], 3)
            early = (gi == 1)
            add("Q%d" % gi, (lambda c=c: gen_Q(c)),
                [g_("P0", gi), g_("R", gi - 1), g_("C", gi - 1) if early else g_("B3", gi - 1)], 4, c)
            add("R%d" % gi, (lambda c=c: gen_R(c)), [g_("Q", gi), g_("B3", gi - 1) if early else None], 6)
            add("C%d" % gi, (lambda c=c: gen_conv(c)), [g_("Q", gi), g_("B3", gi - 1) if early else None], 5, c)
            add("B3%d" % gi, (lambda c=c: gen_B3(c)), [g_("R", gi), g_("C", gi), g_("A4", gi - 1)], 7)
            add("A4%d" % gi, (lambda c=c: gen_phase4(c)), [g_("B3", gi), g_("A5", gi - 1)], 1)
            add("A5%d" % gi, (lambda c=c: gen_phase5(c)), [g_("A4", gi)], 5.5)

        clock = 0.0
        n_emitted = 0
        while True:
            active = [t for t in tasks if not t.done and all(d.done for d in t.deps)]
            if not active:
                break
            ready = [t for t in active if t.ready <= clock]
            order = sorted(ready, key=lambda t: -t.prio) + sorted(
                [t for t in active if t.ready > clock], key=lambda t: t.ready)
            progressed = False
            for t in order:
                if not t.started:
                    use_dve[0] = (clock < 330.0)
                    t.gen = t.gen()
                    t.started = True
                use_dve[0] = (clock < 330.0)
                try:
                    u = next(t.gen)
                except StopIteration:
                    t.done = True
                    progressed = True
                    break
                if u is None:
                    continue
                pe_t, dl = u
                if getattr(t, "c", None) is not None and t.c.sample:
                    pe_t, dl = 0.5, max(dl, 3.5)
                clock = max(clock, t.ready) + pe_t
                t.ready = clock + dl
                n_emitted += 1
                progressed = True
                break
            assert progressed, "scheduler deadlock"

        for d in d_y + [d_cv, d_rp, d_rs, d_ys, d_cvs]:
            pool.wait(d, d.count)
        P.emit(nc, st)
    return nc


_CACHE = {}


def kernel(x_prompt, x_sample, state_conv, state_ret, norm1_g, w_in, conv_w, ret_gn_g, w_out, norm2_g,
           w_gate, w_up, w_down, norm_f_g):
    f = lambda a: np.ascontiguousarray(np.asarray(a, dtype=np.float32))
    x_prompt, x_sample, state_conv, state_ret = f(x_prompt), f(x_sample), f(state_conv), f(state_ret)
    if "nc" not in _CACHE:
        _CACHE["nc"] = build_program()
        _CACHE["consts"] = _const_tables()
    nc = _CACHE["nc"]
    tabs, mask_p, mask_s, m1, m2, _, _ = _CACHE["consts"]
    w_in0 = f(w_in)[0]
    w_in_p = np.ascontiguousarray(np.concatenate(
        [w_in0[:, :2048]] + [w_in0[:, 2048 + 512 * k + 128 * cc:2048 + 512 * k + 128 * (cc + 1)]
                             for cc in range(4) for k in range(3)], axis=1))
    shared = {
        "w_in": w_in_p, "w_out": f(w_out)[0], "w_gate": f(w_gate)[0], "w_up": f(w_up)[0],
        "w_down": f(w_down)[0],
        "g1T": np.ascontiguousarray(f(norm1_g)[0].reshape(8, 128).T),
        "g2T": np.ascontiguousarray(f(norm2_g)[0].reshape(8, 128).T),
        "gngT": np.ascontiguousarray(f(ret_gn_g)[0].reshape(4, 128).T),
        "cwT": np.ascontiguousarray(f(conv_w)[0].reshape(3, 4, 128).transpose(2, 1, 0)),
        "gfb": np.ascontiguousarray(np.broadcast_to(f(norm_f_g)[None, :], (128, D))),
        "tabs": tabs, "mask_p": mask_p, "mask_s": mask_s, "m1": m1, "m2": m2,
        "ident": np.eye(128, dtype=np.float32),
    }
    in_maps = []
    for c in range(NCORES):
        xs = x_sample[c * DEC_PER:(c + 1) * DEC_PER].reshape(DEC_PER * DEC_SEQ, D)
        m = dict(shared)
        m["x"] = np.ascontiguousarray(np.concatenate([x_prompt[c], xs], 0))
        m["sconv"] = np.ascontiguousarray(state_conv[0, c * DEC_PER:(c + 1) * DEC_PER].reshape(2 * DEC_PER, CW))
        m["sret"] = np.ascontiguousarray(state_ret[0, c * DEC_PER:(c + 1) * DEC_PER])
        in_maps.append(m)
    res = run_bass_kernel_spmd(nc, in_maps, core_ids=list(range(NCORES)))
    R = res.results
    y_prompt = np.stack([R[c]["y"][:SEQ] for c in range(NCORES)], 0)
    y_sample = np.concatenate([R[c]["y"][SEQ:].reshape(DEC_PER, DEC_SEQ, D) for c in range(NCORES)], 0)
    convp = np.stack([R[c]["convp"] for c in range(NCORES)], 0)[None]
    retp = np.stack([R[c]["retp"] for c in range(NCORES)], 0)[None]
    convs = np.concatenate([R[c]["convs"].reshape(DEC_PER, 2, CW) for c in range(NCORES)], 0)[None]
    rets = np.concatenate([R[c]["rets"] for c in range(NCORES)], 0)[None]
    return (y_prompt.astype(np.float32), y_sample.astype(np.float32), convp.astype(np.float32),
            retp.astype(np.float32), convs.astype(np.float32), rets.astype(np.float32))
```

```python
import types
import numpy as np
from contextlib import ExitStack
import concourse.bass as bass
import concourse.mybir as mybir
from concourse.bass_utils import run_bass_kernel_spmd

F32 = mybir.dt.float32
BF16 = mybir.dt.bfloat16
AF = mybir.ActivationFunctionType
ALU = mybir.AluOpType

D = 1024
SEQ = 2048
NCORES = 8
DEC_PER = 16
DEC_SEQ = 8
PAST_LEN = 16384
H = 4
HD = 128
CW = 512
DFF = 2816
NFC = 22
INC = 3584
NT = 17
EPS = 1e-6
GN_EPS = 1e-5
NPAN_IN = 8
NPAN_GU = 11
PANW = 4096


def _freeze(fn):
    if fn.__closure__ is None:
        return fn
    cells = []
    for c in fn.__closure__:
        try:
            cells.append(types.CellType(c.cell_contents))
        except ValueError:
            cells.append(c)
    return types.FunctionType(fn.__code__, fn.__globals__, fn.__name__, fn.__defaults__, tuple(cells))


class Eng:
    def __init__(self, name):
        self.name = name
        self.ops = []
        self.count = 0
        self.waited = {}
        self.sem = None

    def wait(self, key, val):
        if key is None or val is None or val <= 0:
            return
        k = id(key)
        if self.waited.get(k, 0) >= val:
            return
        self.waited[k] = val
        self.ops.append(("wait", (key, val)))


class DmaSem:
    def __init__(self, name):
        self.name = name
        self.count = 0
        self.sem = None


class Buf:
    def __init__(self, name, excl=False):
        self.name = name
        self.w = None
        self.r = {}
        self.excl = excl

    def rdeps(self):
        d = [self.w] if self.w else []
        if self.excl:
            d += list(self.r.values())
        return d

    def wdeps(self):
        d = [self.w] if self.w else []
        return d + list(self.r.values())

    def note_r(self, tk):
        self.r[id(tk[0])] = tk

    def note_w(self, tk):
        self.w = tk
        self.r = {}


class Prog:
    def __init__(self):
        self.pe = Eng("pe")
        self.act = Eng("act")
        self.dve = Eng("dve")
        self.pool = Eng("pool")
        self.sp = Eng("sp")
        self.engs = [self.pe, self.act, self.dve, self.pool, self.sp]
        self.dsems = []

    def dsem(self, name):
        d = DmaSem(name)
        self.dsems.append(d)
        return d

    def _deps(self, eng, reads, writes, extra):
        deps = list(extra)
        for b in reads:
            deps += b.rdeps()
        for b in writes:
            deps += b.wdeps()
        for (k, v) in deps:
            eng.wait(k, v)

    def op(self, eng, fn, reads=(), writes=(), extra=()):
        self._deps(eng, reads, writes, extra)
        eng.count += 1
        eng.ops.append(("op", (_freeze(fn), True)))
        tk = (eng, eng.count)
        for b in reads:
            b.note_r(tk)
        for b in writes:
            b.note_w(tk)
        return tk

    def group(self, eng, fns, reads=(), writes=(), extra=()):
        self._deps(eng, reads, writes, extra)
        for fn in fns[:-1]:
            eng.ops.append(("op", (_freeze(fn), False)))
        eng.count += 1
        eng.ops.append(("op", (_freeze(fns[-1]), True)))
        tk = (eng, eng.count)
        for b in reads:
            b.note_r(tk)
        for b in writes:
            b.note_w(tk)
        return tk

    def dma(self, eng, fn, dsem, reads=(), writes=(), extra=()):
        self._deps(eng, reads, writes, extra)
        dsem.count += 16
        eng.ops.append(("dma", (_freeze(fn), dsem)))
        tk = (dsem, dsem.count)
        for b in reads:
            b.note_r(tk)
        for b in writes:
            b.note_w(tk)
        return tk

    def emit(self, nc, stack):
        for e in self.engs:
            e.sem = stack.enter_context(nc.semaphore("s_" + e.name))
        for d in self.dsems:
            d.sem = stack.enter_context(nc.semaphore("d_" + d.name))
        block = stack.enter_context(nc.Block())

        def run(ir):
            def body(e):
                for kind, payload in ir.ops:
                    if kind == "wait":
                        key, val = payload
                        e.wait_ge(key.sem, val)
                    elif kind == "op":
                        fn, signal = payload
                        ins = fn(e)
                        if signal:
                            ins.then_inc(ir.sem, 1)
                    else:
                        fn, dsem = payload
                        fn(e).then_inc(dsem.sem, 16)
            return body

        block.tensor(run(self.pe))
        block.scalar(run(self.act))
        block.vector(run(self.dve))
        block.gpsimd(run(self.pool))
        block.sync(run(self.sp))


def _const_tables():
    half = HD // 2
    inv = (np.float32(10000.0) ** (-(np.arange(half, dtype=np.float32)) / np.float32(half))).astype(np.float32)
    gam = 1.0 - 2.0 ** (-5.0 - np.arange(H, dtype=np.float64))
    tabs = np.zeros((NT, 128, 1536), np.float32)
    p = np.arange(128)
    for gt in range(NT):
        if gt < 16:
            pos = (128 * gt + p).astype(np.float32)
            loc = p.astype(np.float64)
        else:
            pos = (PAST_LEN + (p % DEC_SEQ)).astype(np.float32)
            loc = (p % DEC_SEQ).astype(np.float64)
        ang = (pos[:, None] * inv[None, :]).astype(np.float32).astype(np.float64)
        c = np.cos(ang)
        s = np.sin(ang)
        sq = gam[None, :] ** (loc[:, None] + 1.0)
        sk = gam[None, :] ** (-(loc[:, None] + 1.0)) * (HD ** -0.5)
        CQ = c[:, None, :] * sq[:, :, None]
        CK = c[:, None, :] * sk[:, :, None]
        SQ = np.stack([-s[:, None, :] * sq[:, :, None], s[:, None, :] * sq[:, :, None]], 2)
        SK = np.stack([-s[:, None, :] * sk[:, :, None], s[:, None, :] * sk[:, :, None]], 2)
        tabs[gt] = np.concatenate([CQ.reshape(128, 256), SQ.reshape(128, 512),
                                   CK.reshape(128, 256), SK.reshape(128, 512)], 1).astype(np.float32)
    j = p[:, None]
    i = p[None, :]
    mask_p = (j <= i).astype(np.float32)
    mask_s = ((j <= i) & (j // DEC_SEQ == i // DEC_SEQ)).astype(np.float32)
    m1 = (p[:, None] // DEC_SEQ == np.arange(DEC_PER)[None, :]).astype(np.float32)
    m2 = np.broadcast_to(m1.T[None, :, :], (128, DEC_PER, 128)).astype(np.float32)
    gC_p = (gam ** 128.0).astype(np.float64)
    gC_s = (gam ** float(DEC_SEQ)).astype(np.float64)
    return tabs, mask_p, mask_s, m1, np.ascontiguousarray(m2), gC_p, gC_s


def build_program(groups=None):
    tabs_np, mask_p_np, mask_s_np, m1_np, m2_np, gC_p, gC_s = _const_tables()
    nc = bass.Bass("TRN2", target_bir_lowering=False)
    P = Prog()

    def din(name, shape, dt=F32):
        return nc.dram_tensor(name, list(shape), dt, kind="ExternalInput").ap()

    def dout(name, shape, dt=F32):
        return nc.dram_tensor(name, list(shape), dt, kind="ExternalOutput").ap()

    x_d = din("x", [NT * 128, D])
    sc_d = din("sconv", [2 * DEC_PER, CW])
    sr_d = din("sret", [DEC_PER, H, HD, HD])
    w_in_d = din("w_in", [D, INC])
    w_out_d = din("w_out", [D, D])
    w_gate_d = din("w_gate", [D, DFF])
    w_up_d = din("w_up", [D, DFF])
    w_down_d = din("w_down", [DFF, D])
    g1T_d = din("g1T", [128, 8])
    g2T_d = din("g2T", [128, 8])
    gngT_d = din("gngT", [128, 4])
    cwT_d = din("cwT", [128, 4, 3])
    gfb_d = din("gfb", [128, D])
    tabs_d = din("tabs", [NT, 128, 1536])
    maskp_d = din("mask_p", [128, 128])
    masks_d = din("mask_s", [128, 128])
    m1_d = din("m1", [128, DEC_PER])
    m2_d = din("m2", [128, DEC_PER, 128])
    ident_d = din("ident", [128, 128])

    y_d = dout("y", [NT * 128, D])
    convp_d = dout("convp", [2, CW])
    retp_d = dout("retp", [H, HD, HD])
    convs_d = dout("convs", [2 * DEC_PER, CW])
    rets_d = dout("rets", [DEC_PER, H, HD, HD])

    ws_d = nc.dram_tensor("wscratch", [NPAN_IN + NPAN_GU + 2, 128, PANW], BF16, kind="Internal").ap()

    with ExitStack() as st:
        ARENA_W = 52600
        arena = st.enter_context(nc.sbuf_tensor("arena", [128, ARENA_W], F32))
        cur = [0]

        def alloc(shape, dt=F32):
            n = int(np.prod(shape))
            words = n if dt == F32 else (n + 1) // 2
            words = (words + 7) // 8 * 8
            off = cur[0]
            cur[0] += words
            assert cur[0] <= ARENA_W, ("SBUF arena overflow", cur[0])
            ap = arena[:, off:off + words]
            if dt != F32:
                ap = ap.bitcast(dt)[:, 0:n]
            else:
                ap = ap[:, 0:n]
            if len(shape) == 2:
                ap = ap.rearrange("p (a b) -> p a b", a=shape[0])
            elif len(shape) == 3:
                ap = ap.rearrange("p (a b c) -> p a b c", a=shape[0], b=shape[1])
            return ap

        wpan = [alloc([PANW], BF16) for _ in range(4)]
        wdn = alloc([NFC, D], BF16)
        xres = [alloc([D]) for _ in range(4)]
        xin = [alloc([D]) for _ in range(2)]
        hb0 = alloc([D], BF16)
        hb3 = alloc([D], BF16)
        h1T = alloc([8, 512], BF16)
        h2T = alloc([8, 512], BF16)
        oT = alloc([8, 512], BF16)
        Tst = alloc([H, HD])
        Rbf = alloc([H, HD], BF16)
        junk = alloc([D], BF16)
        gfb = alloc([D])
        sgg = [alloc([512], BF16) for _ in range(2)]
        maskp = alloc([128])
        masks = alloc([128])
        identf = alloc([128])
        identb = alloc([128], BF16)
        g1T = alloc([8])
        g2T = alloc([8])
        gngT = alloc([4])
        cwT = alloc([4, 3])
        mhalf = alloc([8])
        ss = alloc([8])
        ms = alloc([8])
        rstd = alloc([8])
        bnst = alloc([H, 6])
        bnag = alloc([H, 2])
        gvar = alloc([H])
        grstd = alloc([H])
        gnmr = alloc([H])
        m1c = alloc([DEC_PER])
        convT = alloc([32])
        convo = alloc([128])
        sc_tok = alloc([128])
        qk_tok = alloc([4, 2, 512], BF16)
        v_tok = alloc([4, 512], BF16)
        ra = alloc([512])
        rm = alloc([512])
        tabh = [alloc([768]) for _ in range(2)]
        qkT = alloc([8, 128], BF16)
        ST_sb = alloc([H, 128], BF16)
        on_sb = alloc([512], BF16)
        sgT = alloc([4, 512], BF16)
        Ccp = alloc([512])
        ubuf1 = alloc([2 + 512])
        halo = alloc([4, 2 * 1])
        Rf = alloc([DEC_PER // 2, HD])
        qf = alloc([H, 128])
        Vblk = rm.bitcast(BF16).rearrange("p (b v) -> p b v", b=DEC_PER // 2)
        aT = alloc([NFC, 512], BF16)

        banks = [st.enter_context(nc.psum_tensor("pb%d" % i, [128, 512], F32)) for i in range(7)]
        ptb = st.enter_context(nc.psum_tensor("ptb", [128, 1024], BF16))
        DBK = [0, 1, 2, 3]
        BS, BO, BI = 4, 5, 6
        bankB = [Buf("bank%d" % i, excl=True) for i in range(7)]
        ptB = Buf("ptb", excl=True)
        db_i = [0]

        def next_bank():
            i = DBK[db_i[0] % len(DBK)]
            db_i[0] += 1
            return banks[i], bankB[i]

        B = lambda n: Buf(n)
        wpanB = [B("wpan%d" % i) for i in range(4)]
        wdnB = B("wdn")
        xresB = [B("xres%d" % i) for i in range(4)]
        xinB = [B("xin0"), B("xin1")]
        hb0B, hb3B = B("hb0"), B("hb3")
        h1TB = [B("h1T%d" % i) for i in range(4)]
        h2TB = [B("h2T%d" % i) for i in range(4)]
        oTrB = [B("oTr%d" % i) for i in range(4)]
        oTcB = B("oTc")
        TstB, RbfB = B("Tst"), B("Rbf")
        junkB = B("junk")
        constB = B("const")
        sggB = [B("sgg0"), B("sgg1")]
        ssB, msB, rstdB = B("ss"), B("ms"), B("rstd")
        bnB, gnB = B("bn"), B("gn")
        qkB = [[B("qk%d_%d" % (s, j)) for j in range(2)] for s in range(4)]
        vB = [B("v%d" % s) for s in range(4)]
        raB, rmB = B("ra"), B("rm")
        tabB = [B("tab0"), B("tab1")]
        qkTB, STB, onB = B("qkT"), B("ST"), B("on")
        sgTB = B("sgT")
        CcpB = B("Ccp")
        uB1 = B("u")
        haloB = [B("halo%d" % i) for i in range(4)]
        aTB = [B("aT%d" % i) for i in range(NFC)]
        RfB, qfB = B("Rf"), B("qf")
        VblkB = rmB
        gvB, grB, gmB = B("gvar"), B("grstd"), B("gnmr")
        convTB, convoB, scB = B("convT"), B("convo"), B("sc")
        wsB = [B("ws%d" % i) for i in range(NPAN_IN + NPAN_GU + 2)]

        pe, act, dve, pool, sp = P.pe, P.act, P.dve, P.pool, P.sp

        d_const = P.dsem("const")
        d_ws = [P.dsem("ws%d" % i) for i in range(NPAN_IN + NPAN_GU + 2)]
        d_wpan = [P.dsem("wpan%d" % i) for i in range(4)]
        d_wdn = P.dsem("wdn")
        d_x = [P.dsem("x%d" % i) for i in range(4)]
        d_xin = [P.dsem("xin0"), P.dsem("xin1")]
        d_y = [P.dsem("y%d" % i) for i in range(4)]
        xin_i = [0]
        d_tab = [P.dsem("tab0"), P.dsem("tab1")]
        tab_i = [0]
        d_misc = P.dsem("misc")
        d_cv, d_rp = P.dsem("cv"), P.dsem("rp")
        d_rf, d_rs = P.dsem("rf"), P.dsem("rs")
        d_ys, d_cvs = P.dsem("ys"), P.dsem("cvs")

        def cload(dst, src):
            P.dma(sp, lambda e: e.dma_start(out=dst, in_=src), d_const, writes=[constB])

        cload(identf, ident_d[:, :])
        cload(maskp, maskp_d[:, :])
        cload(masks, masks_d[:, :])
        cload(g1T, g1T_d[:, :])
        cload(g2T, g2T_d[:, :])
        cload(gngT, gngT_d[:, :])
        cload(cwT, cwT_d[:, :, :])
        cload(gfb, gfb_d[:, :])
        cload(m1c, m1_d[:, :])
        identbB, mhB = B("identb"), B("mhalf")
        P.op(dve, lambda e: e.tensor_copy(out=identb, in_=identf), reads=[constB], writes=[identbB])
        P.op(pool, lambda e: e.memset(mhalf, -0.5), writes=[mhB])

        def cast_in_panel(j):
            dst = ws_d[j]
            if j < 4:
                P.dma(pool, lambda e: e.dma_start(
                    out=dst.rearrange("p (kc n) -> p kc n", kc=8),
                    in_=w_in_d[:, j * 512:(j + 1) * 512].rearrange("(kc p) n -> p kc n", p=128)),
                    d_ws[j], writes=[wsB[j]])
            else:
                cc = j - 4
                dv = dst[:, 0:8 * 384].rearrange("p (kc n) -> p kc n", kc=8)
                c0 = 2048 + 384 * cc
                P.dma(pool, lambda e: e.dma_start(
                    out=dv, in_=w_in_d[:, c0:c0 + 384].rearrange("(kc p) n -> p kc n", p=128)),
                    d_ws[j], writes=[wsB[j]])

        def cast_gu_panel(j):
            dst = ws_d[NPAN_IN + j].rearrange("p (kc t n) -> p kc t n", kc=8, t=2)
            for t, wd in enumerate((w_gate_d, w_up_d)):
                P.dma(pool, lambda e, t=t, wd=wd: e.dma_start(
                    out=dst[:, :, t, :],
                    in_=wd[:, j * 256:(j + 1) * 256].rearrange("(kc p) n -> p kc n", p=128)),
                    d_ws[NPAN_IN + j], writes=[wsB[NPAN_IN + j]])

        pool.wait(d_const, d_const.count)
        for j in range(NPAN_IN):
            cast_in_panel(j)
        for hh in range(2):
            jo = NPAN_IN + NPAN_GU + hh
            P.dma(pool, lambda e: e.dma_start(
                out=ws_d[jo].rearrange("p (kc n) -> p kc n", kc=8),
                in_=w_out_d[:, hh * 512:(hh + 1) * 512].rearrange("(kc p) n -> p kc n", p=128)),
                d_ws[jo], writes=[wsB[jo]])
        for j in range(NPAN_GU):
            cast_gu_panel(j)
        P.dma(pool, lambda e: e.dma_start(out=wdn, in_=w_down_d.rearrange("(fc p) n -> p fc n", p=128)),
              d_wdn, writes=[wdnB])
        pending_casts = []

        def issue_casts(n):
            for _ in range(n):
                if pending_casts:
                    pending_casts.pop(0)()

        pan_i = {"A": 0, "B": 0}

        def load_panel(idx, width=PANW, pool_="B"):
            k = pan_i[pool_] % 2 + (0 if pool_ == "A" else 2)
            pan_i[pool_] += 1
            P.dma(sp, lambda e: e.dma_start(out=wpan[k][:, 0:width], in_=ws_d[idx][:, 0:width]), d_wpan[k],
                  reads=[wsB[idx]], writes=[wpanB[k]])
            return wpan[k], wpanB[k]

        class PanelStream:
            def __init__(self, specs, pool_="B", ahead=True):
                self.specs, self.pool_, self.got, self.ahead = specs, pool_, {}, ahead

            def _ld(self, i):
                if i < len(self.specs) and i not in self.got:
                    idx, width = self.specs[i]
                    self.got[i] = load_panel(idx, width, self.pool_)

            def get(self, i):
                self._ld(i)
                if self.ahead:
                    self._ld(i + 1)
                return self.got[i]

        I32 = mybir.dt.int32
        nt1 = alloc([8])
        ntB = B("nt1")
        use_dve = [True]

        def rsq(x_ap, out_ap, xB, outB):
            if not use_dve[0]:
                k = x_ap.shape[1]
                P.op(pool, lambda e: e.tensor_tensor(out=out_ap, in0=x_ap, in1=mhalf[:, 0:k], op=ALU.pow),
                     reads=[xB, mhB], writes=[outB])
                return
            k = x_ap.shape[1]
            t1 = nt1[:, 0:k]
            P.op(dve, lambda e: e.tensor_scalar(out=out_ap.bitcast(I32), in0=x_ap.bitcast(I32), scalar1=-0.5,
                                                scalar2=float(0x5f3759df), op0=ALU.mult, op1=ALU.add),
                 reads=[xB], writes=[outB])
            for _ in range(2):
                if k == 1:
                    P.op(dve, lambda e: e.scalar_tensor_tensor(out=t1, in0=out_ap, scalar=x_ap[:, 0:1], in1=out_ap,
                                                               op0=ALU.mult, op1=ALU.mult),
                         reads=[outB, xB], writes=[ntB])
                else:
                    P.op(dve, lambda e: e.tensor_tensor(out=t1, in0=out_ap, in1=out_ap, op=ALU.mult),
                         reads=[outB], writes=[ntB])
                    P.op(dve, lambda e: e.tensor_tensor(out=t1, in0=t1, in1=x_ap, op=ALU.mult),
                         reads=[ntB, xB], writes=[ntB])
                P.op(dve, lambda e: e.tensor_scalar(out=t1, in0=t1, scalar1=-0.5, scalar2=1.5, op0=ALU.mult,
                                                    op1=ALU.add), reads=[ntB], writes=[ntB])
                P.op(dve, lambda e: e.tensor_tensor(out=out_ap, in0=out_ap, in1=t1, op=ALU.mult),
                     reads=[outB, ntB], writes=[outB])

        def rms_sq(src_ap, srcB, col):
            P.op(act, lambda e: e.activation(out=junk, in_=src_ap, func=AF.Square, scale=1.0 / 32.0,
                                             accum_out=ss[:, col:col + 1]),
                 reads=[srcB], writes=[ssB])

        def rms_rs(col):
            P.op(dve, lambda e: e.tensor_scalar_add(out=ms[:, col:col + 1], in0=ss[:, col:col + 1], scalar1=EPS),
                 reads=[ssB], writes=[msB])
            rsq(ms[:, col:col + 1], rstd[:, col:col + 1], msB, rstdB)

        def to_featmajor(s, gT, hT, hTB, hb, hbB):
            ptv = ptb[:, :].rearrange("p (a b) -> p a b", a=8)
            P.group(pe, [lambda e, kc=kc: e.transpose(out=ptv[:, kc, :], in_=hb[:, kc * 128:(kc + 1) * 128],
                                                     identity=identb) for kc in range(8)],
                    reads=[hbB, identbB], writes=[ptB])
            P.op(dve, lambda e: e.tensor_tensor(out=hT[:, :, s * 128:(s + 1) * 128], in0=ptv,
                                                in1=gT.unsqueeze(2).to_broadcast([128, 8, 128]), op=ALU.mult),
                 reads=[ptB, constB], writes=[hTB[s]])

        if groups is None:
            groups = [[0, 1, 2, 3], [4, 5, 6, 7], [8, 9, 10, 11], [12, 13, 14, 15], [16]]

        class Ctx:
            pass

        ctxs = []
        for gi, tiles in enumerate(groups):
            c = Ctx()
            c.gi, c.tiles, c.T, c.N = gi, tiles, len(tiles), 128 * len(tiles)
            c.sample = (tiles[0] == 16)
            c.nseq, c.L = (DEC_PER, DEC_SEQ) if c.sample else (1, c.N)
            c.gC = gC_s if c.sample else gC_p
            c.mask = masks if c.sample else maskp
            c.xs = list(range(c.T))
            ctxs.append(c)

        ptv = ptb[:, :].rearrange("p (a b) -> p a b", a=8)

        def phase0(c, s, part):
            gt = c.tiles[s]
            if part == "load":
                k = xin_i[0] % 2
                xin_i[0] += 1
                c.xk[s] = k
                P.dma(sp, lambda e: e.dma_start(out=xin[k], in_=x_d[gt * 128:(gt + 1) * 128, :]),
                      d_xin[k], writes=[xinB[k]])
            elif part == "sq":
                k = c.xk[s]
                rms_sq(xin[k], xinB[k], 0)
            elif part == 0:
                k = c.xk[s]
                rms_rs(0)
                P.op(dve, lambda e: e.tensor_scalar(out=hb0, in0=xin[k], scalar1=rstd[:, 0:1], scalar2=None,
                                                    op0=ALU.mult),
                     reads=[xinB[k], rstdB], writes=[hb0B])
            else:
                to_featmajor(s, g1T, h1T, h1TB, hb0, hb0B)

        def gen_phase1a(c):
            c.qstream = PanelStream([(0, PANW), (1, PANW), (2, PANW), (3, PANW)], ahead=c.sample)
            for blk in range(3):
                pan, panB = c.qstream.get(blk)
                panv = pan.rearrange("p (kc n) -> p kc n", kc=8)
                for s, gt in enumerate(c.tiles):
                    bk, bkB = next_bank()
                    P.group(pe, [lambda e, kc=kc: e.matmul(
                        bk[:, :], lhsT=h1T[:, kc, s * 128:(s + 1) * 128], rhs=panv[:, kc, :],
                        start=(kc == 0), stop=(kc == 7)) for kc in range(8)],
                        reads=[h1TB[s], panB], writes=[bkB])
                    if blk == 2:
                        P.op(act, lambda e: e.activation(out=v_tok[:, s, :], in_=bk[:, :], func=AF.Copy),
                             reads=[bkB], writes=[vB[s]])
                        yield (1.8, 0.0)
                        continue
                    c_off = 0 if blk == 0 else 768
                    tk_ = tab_i[0] % 2
                    tab_i[0] += 1
                    tab, tabB_ = tabh[tk_], tabB[tk_]
                    P.dma(sp, lambda e: e.dma_start(out=tab, in_=tabs_d[gt][:, c_off:c_off + 768]),
                          d_tab[tk_], writes=[tabB_])
                    Ct = tab[:, 0:256].rearrange("p (h d) -> p h d", h=4)
                    St = tab[:, 256:768].rearrange("p (h t d) -> p h t d", h=4, t=2)
                    ps4 = bk[:, :].rearrange("p (h t d) -> p h t d", h=4, t=2)
                    ra4 = ra.rearrange("p (h t d) -> p h t d", h=4, t=2)
                    rm4 = rm.rearrange("p (h t d) -> p h t d", h=4, t=2)
                    P.op(dve, lambda e: e.tensor_tensor(
                        out=ra4, in0=ps4, in1=Ct.unsqueeze(2).to_broadcast([128, 4, 2, 64]), op=ALU.mult),
                        reads=[bkB, tabB_], writes=[raB])
                    P.op(dve, lambda e: e.tensor_tensor(
                        out=rm4[:, :, 0, :], in0=ps4[:, :, 1, :], in1=St[:, :, 0, :], op=ALU.mult),
                        reads=[bkB, tabB_], writes=[rmB])
                    P.op(dve, lambda e: e.tensor_tensor(
                        out=rm4[:, :, 1, :], in0=ps4[:, :, 0, :], in1=St[:, :, 1, :], op=ALU.mult),
                        reads=[bkB, tabB_], writes=[rmB])
                    yield (1.8, 2.4)
                    P.op(dve if use_dve[0] else pool,
                         lambda e: e.tensor_tensor(out=qk_tok[:, s, blk, :], in0=ra, in1=rm, op=ALU.add),
                         reads=[raB, rmB], writes=[qkB[s][blk]])
                    yield (0.0, 1.5)
                if c.gi == 0:
                    issue_casts(1)

        def gen_gpanel(c):
            N, T = c.N, c.T
            pan, panB = c.qstream.get(3)
            panv = pan.rearrange("p (kc n) -> p kc n", kc=8)
            for cc in range(4):
                bk, bkB = next_bank()
                P.group(pe, [lambda e, kc=kc: e.matmul(
                    bk[:, 0:N], lhsT=panv[:, kc, cc * 128:(cc + 1) * 128], rhs=h1T[:, kc, 0:N],
                    start=(kc == 0), stop=(kc == 7)) for kc in range(8)],
                    reads=h1TB[:T] + [panB], writes=[bkB])
                P.op(act, lambda e: e.activation(out=sgT[:, cc, 0:N], in_=bk[:, 0:N], func=AF.Silu),
                     reads=[bkB], writes=[sgTB])
                yield (1.8, 0.0)
            if c.gi == 0:
                issue_casts(1)

        def gen_conv(c):
            N, T, nseq, L, sample = c.N, c.T, c.nseq, c.L, c.sample
            v3 = lambda ap: ap.rearrange("p (b l) -> p b l", b=nseq)
            if sample:
                P.dma(sp, lambda e: e.dma_start(out=sc_tok[0:32, :], in_=sc_d[:, 0:128]), d_misc, writes=[scB])
            cstream = PanelStream([(4 + i, 8 * 384) for i in range(4)], ahead=sample)
            for cc in range(4):
                pan, panB = cstream.get(cc)
                panv = pan[:, 0:8 * 384].rearrange("p (kc n) -> p kc n", kc=8)
                u = ubuf1
                u3 = u[:, 0:nseq * (2 + L)].rearrange("p (b l) -> p b l", b=nseq)
                if c.tiles[0] == 0:
                    P.op(dve, lambda e: e.memset(u[:, 0:2], 0.0), writes=[uB1])
                elif not sample:
                    P.op(dve, lambda e: e.tensor_copy(out=u[:, 0:2], in_=halo[:, cc, :]),
                         reads=[haloB[cc]], writes=[uB1])
                if sample:
                    bk, bkB = next_bank()
                    P.op(pe, lambda e: e.matmul(
                        bk[:, 0:32], lhsT=sc_tok[0:32, :], rhs=identf[0:32, 0:32],
                        start=True, stop=True), reads=[scB, constB], writes=[bkB])
                    if cc + 1 < 4:
                        P.dma(sp, lambda e: e.dma_start(out=sc_tok[0:32, :],
                                                        in_=sc_d[:, (cc + 1) * 128:(cc + 2) * 128]),
                              d_misc, writes=[scB])
                    P.op(dve, lambda e: e.tensor_copy(
                        out=u3[:, :, 0:2], in_=bk[:, 0:32].rearrange("p (b j) -> p b j", b=DEC_PER)),
                        reads=[bkB], writes=[uB1])
                bks = {}
                for k in (1, 2, 0):
                    bk, bkB_ = next_bank()
                    bks[k] = (bk, bkB_)
                    P.group(pe, [lambda e, kc=kc: e.matmul(
                        bk[:, 0:N], lhsT=panv[:, kc, k * 128:(k + 1) * 128], rhs=h1T[:, kc, 0:N],
                        start=(kc == 0), stop=(kc == 7)) for kc in range(8)],
                        reads=h1TB[:T] + [panB], writes=[bkB_])
                    if k == 1:
                        bkC, bkCB = bk, bkB_
                        P.op(act, lambda e: e.activation(out=Ccp[:, 0:N], in_=bkC[:, 0:N], func=AF.Copy),
                             reads=[bkCB], writes=[CcpB])
                    elif k == 2:
                        bkX, bkXB = bk, bkB_
                        P.op(dve, lambda e: e.tensor_tensor(
                            out=u3[:, :, 2:2 + L], in0=v3(bkX[:, 0:N]), in1=v3(Ccp[:, 0:N]), op=ALU.mult),
                            reads=[bkXB, CcpB], writes=[uB1])
                        ca3 = v3(Ccp[:, 0:N])
                        P.op(dve, lambda e: e.tensor_scalar(
                            out=ca3, in0=u3[:, :, 2:2 + L], scalar1=cwT[:, cc, 2:3], scalar2=None, op0=ALU.mult),
                            reads=[uB1, constB], writes=[CcpB])
                        for jj in (1, 0):
                            P.op(dve, lambda e: e.scalar_tensor_tensor(
                                out=ca3, in0=u3[:, :, jj:jj + L], scalar=cwT[:, cc, jj:jj + 1], in1=ca3,
                                op0=ALU.mult, op1=ALU.add),
                                reads=[uB1, CcpB, constB], writes=[CcpB])
                    else:
                        bkG, bkGB = bk, bkB_
                        P.op(dve, lambda e: e.tensor_tensor(
                            out=oT[:, 4 + cc, 0:N], in0=bkG[:, 0:N], in1=Ccp[:, 0:N], op=ALU.mult),
                            reads=[bkGB, CcpB], writes=[oTcB])
                    yield (1.8, 2.0)
                if c.tiles[-1] == 15 or sample:
                    nr = 2 * nseq
                    P.op(dve, lambda e: e.tensor_copy(
                        out=convT[:, 0:nr].rearrange("p (b j) -> p b j", b=nseq), in_=u3[:, :, L:L + 2]),
                        reads=[uB1], writes=[convTB])
                    yield (0.0, 4.0)
                    bk, bkB = next_bank()
                    P.op(pe, lambda e: e.matmul(bk[0:nr, 0:128], lhsT=convT[:, 0:nr], rhs=identf,
                                                start=True, stop=True),
                         reads=[convTB, constB], writes=[bkB])
                    P.op(dve, lambda e: e.tensor_copy(out=convo[0:nr, :], in_=bk[0:nr, 0:128]),
                         reads=[bkB], writes=[convoB])
                    dst = convs_d if sample else convp_d
                    P.dma(sp if sample else pool,
                          lambda e: e.dma_start(out=dst[:, cc * 128:(cc + 1) * 128], in_=convo[0:nr, :]),
                          d_cvs if sample else d_cv, reads=[convoB])
                else:
                    P.op(dve, lambda e: e.tensor_copy(out=halo[:, cc, :], in_=u[:, N:N + 2]),
                         reads=[uB1], writes=[haloB[cc]])
                if c.gi == 0:
                    issue_casts(1)

        def gen_ret(c, s):
            gt = c.tiles[s]
            sample, gC, mask = c.sample, c.gC, c.mask
            first = (gt == 0)
            deferred_act = None
            P.group(pe, [lambda e, a=a: e.transpose(
                out=ptv[:, a, :], in_=qk_tok[:, s, a // 4, (a % 4) * 128:(a % 4 + 1) * 128], identity=identb)
                for a in range(8)],
                reads=[qkB[s][0], qkB[s][1], identbB], writes=[ptB])
            P.op(dve, lambda e: e.tensor_copy(out=qkT, in_=ptv), reads=[ptB], writes=[qkTB])
            yield (0.9, 2.5)
            Sv = banks[BS][:, :].rearrange("p (h i) -> p h i", h=4)
            P.group(pe, [lambda e, h=h: e.matmul(Sv[:, h, :], lhsT=qkT[:, 4 + h, :], rhs=qkT[:, h, :],
                                                 start=True, stop=True) for h in range(4)],
                    reads=[qkTB], writes=[bankB[BS]])
            P.op(dve, lambda e: e.tensor_tensor(
                out=ST_sb, in0=Sv, in1=mask.unsqueeze(1).to_broadcast([128, 4, 128]), op=ALU.mult),
                reads=[bankB[BS], constB], writes=[STB])
            yield (0.5, 2.5)
            Ov = banks[BO][:, :].rearrange("p (h v) -> p h v", h=4)
            Iv = banks[BI][:, :].rearrange("p (h v) -> p h v", h=4)
            if not sample:
                fns = []
                for h in range(4):
                    fns.append(lambda e, h=h: e.matmul(
                        Ov[:, h, :], lhsT=ST_sb[:, h, :], rhs=v_tok[:, s, h * 128:(h + 1) * 128],
                        start=True, stop=first))
                    if not first:
                        fns.append(lambda e, h=h: e.matmul(Ov[:, h, :], lhsT=qkT[:, h, :], rhs=Rbf[:, h, :],
                                                           start=False, stop=True))
                P.group(pe, fns, reads=[STB, vB[s], qkTB] + ([] if first else [RbfB]), writes=[bankB[BO]])
                P.group(pe, [lambda e, h=h: e.matmul(
                    Iv[:, h, :], lhsT=qk_tok[:, s, 1, h * 128:(h + 1) * 128],
                    rhs=v_tok[:, s, h * 128:(h + 1) * 128], start=True, stop=True) for h in range(4)],
                    reads=[qkB[s][1], vB[s]], writes=[bankB[BI]])
                yield (1.3, 1.2)
                if first:
                    P.op(dve, lambda e: e.tensor_copy(out=Tst, in_=Iv), reads=[bankB[BI]], writes=[TstB])
                else:
                    for h in range(4):
                        P.op(dve, lambda e, h=h: e.scalar_tensor_tensor(
                            out=Tst[:, h, :], in0=Tst[:, h, :], scalar=float(gC[h]), in1=Iv[:, h, :],
                            op0=ALU.mult, op1=ALU.add),
                            reads=[bankB[BI], TstB], writes=[TstB])
                if gt < 15:
                    def rbf_ops():
                        for h in range(4):
                            P.op(act, lambda e, h=h: e.activation(out=Rbf[:, h, :], in_=Tst[:, h, :], func=AF.Copy,
                                                                  scale=float(gC[h])),
                                 reads=[TstB], writes=[RbfB])
                    deferred_act = rbf_ops
                else:
                    for h in range(4):
                        P.op(act, lambda e, h=h: e.activation(out=Tst[:, h, :], in_=Tst[:, h, :], func=AF.Copy,
                                                              scale=float(gC[h])),
                             reads=[TstB], writes=[TstB])
                    P.dma(pool, lambda e: e.dma_start(out=retp_d.rearrange("h d v -> d h v"), in_=Tst),
                          d_rp, reads=[TstB])
            else:
                XTv = banks[BS][:, :].rearrange("p (h i) -> p h i", h=4)
                HB = DEC_PER // 2
                P.op(act, lambda e: e.activation(out=qf, in_=qkT[:, 0:4, :], func=AF.Copy),
                     reads=[qkTB], writes=[qfB])
                Vb = [(Vblk, VblkB), (ra.bitcast(BF16).rearrange("p (b v) -> p b v", b=HB), raB)]
                its = [(h, hf) for h in range(4) for hf in range(2)]

                def mk_vblk(i):
                    h_, hf_ = its[i]
                    vb, vbB = Vb[i % 2]
                    P.op(dve if use_dve[0] else pool, lambda e: e.tensor_tensor(
                        out=vb, in0=v_tok[:, s, h_ * 128:(h_ + 1) * 128].unsqueeze(1).to_broadcast([128, HB, 128]),
                        in1=m1c[:, hf_ * HB:(hf_ + 1) * HB].unsqueeze(2).to_broadcast([128, HB, 128]), op=ALU.mult),
                        reads=[vB[s], constB], writes=[vbB])

                def ld_rf(i):
                    h_, hf_ = its[i]
                    P.dma(pool, lambda e: e.dma_start(
                        out=Rf, in_=sr_d[hf_ * HB:(hf_ + 1) * HB, h_, :, :].rearrange("b d v -> d b v")),
                        d_rf, writes=[RfB])

                mk_vblk(0)
                ld_rf(0)
                for i, (h, hf) in enumerate(its):
                    b0 = hf * HB
                    vb, vbB = Vb[i % 2]
                    P.group(pe, [lambda e, bl=bl: e.matmul(
                        XTv[:, h, (b0 + bl) * DEC_SEQ:(b0 + bl + 1) * DEC_SEQ], lhsT=Rf[:, bl, :],
                        rhs=qf[:, h, (b0 + bl) * DEC_SEQ:(b0 + bl + 1) * DEC_SEQ], start=True, stop=True)
                        for bl in range(HB)],
                        reads=[RfB, qfB, STB], writes=[bankB[BS]])
                    bk2 = [next_bank(), next_bank()]
                    for q4 in range(2):
                        bk, bkB = bk2[q4]
                        P.op(pe, lambda e: e.matmul(
                            bk[:, :], lhsT=qk_tok[:, s, 1, h * 128:(h + 1) * 128],
                            rhs=vb[:, 4 * q4:4 * q4 + 4, :], start=True, stop=True),
                            reads=[qkB[s][1], vbB], writes=[bkB])
                    if i + 1 < len(its):
                        mk_vblk(i + 1)
                    P.op(dve, lambda e: e.tensor_scalar(out=Rf, in0=Rf, scalar1=float(gC[h]), scalar2=None,
                                                        op0=ALU.mult), reads=[RfB], writes=[RfB])
                    for q4 in range(2):
                        bk, bkB = bk2[q4]
                        P.op(dve, lambda e: e.scalar_tensor_tensor(
                            out=Rf[:, 4 * q4:4 * q4 + 4, :], in0=bk[:, :].rearrange("p (b v) -> p b v", b=4),
                            scalar=float(gC[h]), in1=Rf[:, 4 * q4:4 * q4 + 4, :], op0=ALU.mult, op1=ALU.add),
                            reads=[bkB, RfB], writes=[RfB])
                    P.dma(pool, lambda e: e.dma_start(
                        out=rets_d[b0:b0 + HB, h, :, :].rearrange("b d v -> d b v"), in_=Rf), d_rs, reads=[RfB])
                    if i + 1 < len(its):
                        ld_rf(i + 1)
                    yield (1.0, 8.0)
                P.op(act, lambda e: e.activation(out=ra, in_=banks[BS][:, :], func=AF.Copy),
                     reads=[bankB[BS]], writes=[raB])
                fns = []
                for h in range(4):
                    fns.append(lambda e, h=h: e.matmul(
                        Ov[:, h, :], lhsT=ST_sb[:, h, :], rhs=v_tok[:, s, h * 128:(h + 1) * 128],
                        start=True, stop=False))
                    fns.append(lambda e, h=h: e.matmul(
                        Ov[:, h, :], lhsT=ra[:, h * 128:(h + 1) * 128], rhs=identf, start=False, stop=True))
                P.group(pe, fns, reads=[STB, vB[s], raB, constB], writes=[bankB[BO]])
            pass
            for h in range(4):
                P.op(dve, lambda e, h=h: e.bn_stats(out=bnst[:, h, :], in_=Ov[:, h, :]),
                     reads=[bankB[BO]], writes=[bnB])
            for h in range(4):
                P.op(dve, lambda e, h=h: e.bn_aggr(out=bnag[:, h, :], in_=bnst[:, h, :]),
                     reads=[bnB], writes=[gnB])
            P.op(dve, lambda e: e.tensor_scalar_add(out=gvar, in0=bnag[:, :, 1], scalar1=GN_EPS),
                 reads=[gnB], writes=[gvB])
            yield (0.0 if not sample else 1.3, 3.5)
            rsq(gvar, grstd, gvB, grB)
            P.op(dve, lambda e: e.scalar_tensor_tensor(out=gnmr, in0=bnag[:, :, 0], scalar=-1.0, in1=grstd,
                                                       op0=ALU.mult, op1=ALU.mult),
                 reads=[gnB, grB], writes=[gmB])
            yield (0.0, 2.5)
            if deferred_act is not None:
                deferred_act()
            for h in range(4):
                P.op(act, lambda e, h=h: e.activation(
                    out=on_sb[:, h * 128:(h + 1) * 128], in_=Ov[:, h, :], func=AF.Identity,
                    scale=grstd[:, h:h + 1], bias=gnmr[:, h:h + 1]),
                    reads=[bankB[BO], grB, gmB], writes=[onB])
            yield (0.0, 4.0)
            P.group(pe, [lambda e, h=h: e.transpose(out=ptv[:, h, :], in_=on_sb[:, h * 128:(h + 1) * 128],
                                                    identity=identb) for h in range(4)],
                    reads=[onB, identbB], writes=[ptB])
            for h in range(4):
                P.op(dve, lambda e, h=h: e.scalar_tensor_tensor(
                    out=oT[:, h, s * 128:(s + 1) * 128], in0=ptv[:, h, :], scalar=gngT[:, h:h + 1],
                    in1=sgT[:, h, s * 128:(s + 1) * 128], op0=ALU.mult, op1=ALU.mult),
                    reads=[ptB, sgTB, constB], writes=[oTrB[s]])
            yield (0.5, 0.5)

        def gen_phase3(c, s, wo):
            xs = c.xs[s]
            gt = c.tiles[s]
            P.dma(sp, lambda e: e.dma_start(out=xres[xs], in_=x_d[gt * 128:(gt + 1) * 128, :]),
                  d_x[xs], writes=[xresB[xs]])
            for half in range(2):
                pan, panB = wo[half]
                panv = pan.rearrange("p (kc n) -> p kc n", kc=8)
                bk, bkB = next_bank()
                P.group(pe, [lambda e, kc=kc: e.matmul(
                    bk[:, :], lhsT=oT[:, kc, s * 128:(s + 1) * 128], rhs=panv[:, kc, :],
                    start=(kc == 0), stop=(kc == 7)) for kc in range(8)],
                    reads=[oTrB[s], oTcB, panB], writes=[bkB])
                P.op(dve, lambda e: e.tensor_tensor(
                    out=xres[xs][:, half * 512:(half + 1) * 512], in0=bk[:, :],
                    in1=xres[xs][:, half * 512:(half + 1) * 512], op=ALU.add),
                    reads=[bkB, xresB[xs]], writes=[xresB[xs]])
                yield (1.8, 0.0) if half == 0 else (1.8, 2.5)
            rms_sq(xres[xs], xresB[xs], 1)
            yield (0.0, 1.5)
            rms_rs(1)
            P.op(dve, lambda e: e.tensor_scalar(out=hb3, in0=xres[xs], scalar1=rstd[:, 1:2], scalar2=None,
                                                op0=ALU.mult),
                 reads=[xresB[xs], rstdB], writes=[hb3B])
            yield (0.0, 4.5)
            to_featmajor(s, g2T, h2T, h2TB, hb3, hb3B)
            yield (0.9, 1.2)

        def load_wout(pool_="B"):
            return [load_panel(NPAN_IN + NPAN_GU + hh, PANW, pool_) for hh in range(2)]

        def gen_phase4(c):
            N, T = c.N, c.T
            if c.sample:
                last_group = (c is ctxs[-1])
                for j in range(NPAN_GU):
                    pan, panB = load_panel(NPAN_IN + j, PANW, "B" if (last_group and (j // 2) % 2 == 1) else "A")
                    panv = pan.rearrange("p (kc t n) -> p kc t n", kc=8, t=2)
                    k2 = j % 2
                    bks = []
                    for t in range(2):
                        bk, bkB_ = next_bank()
                        bks.append((bk, bkB_))
                        P.group(pe, [lambda e, kc=kc: e.matmul(
                            bk[:, 0:256], lhsT=h2T[:, kc, 0:128], rhs=panv[:, kc, t, :],
                            start=(kc == 0), stop=(kc == 7)) for kc in range(8)],
                            reads=[h2TB[0], panB], writes=[bkB_])
                    (bkg, bkgB), (bku, bkuB) = bks
                    P.op(act, lambda e: e.activation(out=sgg[k2][:, 0:256], in_=bkg[:, 0:256], func=AF.Silu),
                         reads=[bkgB], writes=[sggB[k2]])
                    P.op(dve, lambda e: e.tensor_tensor(out=sgg[k2][:, 0:256], in0=bku[:, 0:256],
                                                        in1=sgg[k2][:, 0:256], op=ALU.mult),
                         reads=[bkuB, sggB[k2]], writes=[sggB[k2]])
                    yield (1.9, 2.5)
                    P.group(pe, [lambda e, fl=fl: e.transpose(
                        out=ptv[:, fl, :], in_=sgg[k2][:, fl * 128:(fl + 1) * 128], identity=identb)
                        for fl in range(2)], reads=[sggB[k2], identbB], writes=[ptB])
                    P.op(dve, lambda e: e.tensor_copy(out=aT[:, 2 * j:2 * j + 2, 0:128], in_=ptv[:, 0:2, :]),
                         reads=[ptB], writes=[aTB[2 * j], aTB[2 * j + 1]])
                    yield (0.3, 1.2)
                return
            for j in range(NPAN_GU):
                pan, panB = load_panel(NPAN_IN + j, PANW, "A")
                panv = pan.rearrange("p (kc t n) -> p kc t n", kc=8, t=2)
                for fl in range(2):
                    fc = 2 * j + fl
                    k2 = fc % 2
                    for t in range(2):
                        bk, bkB_ = next_bank()
                        P.group(pe, [lambda e, kc=kc: e.matmul(
                            bk[:, 0:N], lhsT=panv[:, kc, t, fl * 128:(fl + 1) * 128], rhs=h2T[:, kc, 0:N],
                            start=(kc == 0), stop=(kc == 7)) for kc in range(8)],
                            reads=h2TB[:T] + [panB], writes=[bkB_])
                        if t == 0:
                            P.op(act, lambda e: e.activation(out=sgg[k2][:, 0:N], in_=bk[:, 0:N], func=AF.Silu),
                                 reads=[bkB_], writes=[sggB[k2]])
                        else:
                            P.op(dve, lambda e: e.tensor_tensor(
                                out=aT[:, fc, 0:N], in0=bk[:, 0:N], in1=sgg[k2][:, 0:N], op=ALU.mult),
                                reads=[bkB_, sggB[k2]], writes=[aTB[fc]])
                        yield (0.45 * T, 0.0)

        def gen_phase5(c):
            for s, gt in enumerate(c.tiles):
                xs = c.xs[s]
                for half in range(2):
                    bk, bkB = next_bank()
                    P.group(pe, [lambda e, fc=fc: e.matmul(
                        bk[:, :], lhsT=aT[:, fc, s * 128:(s + 1) * 128], rhs=wdn[:, fc, half * 512:(half + 1) * 512],
                        start=(fc == 0), stop=(fc == NFC - 1)) for fc in range(NFC)],
                        reads=aTB + [wdnB], writes=[bkB])
                    P.op(dve, lambda e: e.tensor_tensor(
                        out=xres[xs][:, half * 512:(half + 1) * 512], in0=bk[:, :],
                        in1=xres[xs][:, half * 512:(half + 1) * 512], op=ALU.add),
                        reads=[bkB, xresB[xs]], writes=[xresB[xs]])
                    if half == 1:
                        yield (4.9, 1.5)
                        rms_sq(xres[xs], xresB[xs], 2)
                        yield (0.0, 1.5)
                        rms_rs(2)
                        P.op(dve, lambda e: e.scalar_tensor_tensor(
                            out=xres[xs], in0=xres[xs], scalar=rstd[:, 2:3], in1=gfb, op0=ALU.mult, op1=ALU.mult),
                            reads=[xresB[xs], rstdB, constB], writes=[xresB[xs]])
                        P.dma(sp if c.sample else pool,
                              lambda e: e.dma_start(out=y_d[gt * 128:(gt + 1) * 128, :], in_=xres[xs]),
                              d_ys if c.sample else d_y[xs], reads=[xresB[xs]])
                        c.a5_done = s + 1
                        yield (0.0, 0.0)
                    else:
                        yield (4.9, 0.0)

        def step(g):
            try:
                next(g)
                return True
            except StopIteration:
                return False

        def gen_P0(c):
            c.xk = {}
            for s_ in range(c.T):
                phase0(c, s_, "load")
                yield (0.0, 3.5)
                phase0(c, s_, "sq")
                yield (0.0, 1.5)
                phase0(c, s_, 0)
                yield (0.0, 4.0)
                phase0(c, s_, 1)
                yield (0.9, 1.2)

        def gen_Q(c):
            for u in gen_phase1a(c):
                yield u
            for u in gen_gpanel(c):
                yield u

        def gen_R(c):
            for s_ in range(c.T):
                for u in gen_ret(c, s_):
                    yield u

        def gen_B3(c):
            wo = load_wout("A" if c.gi == 0 else "B")
            prev = c.prev
            for s_ in range(c.T):
                while prev is not None and s_ < prev.T and getattr(prev, "a5_done", 0) <= s_:
                    yield None
                for u in gen_phase3(c, s_, wo):
                    yield u

        class Task:
            def __init__(self, name, gen, deps, prio):
                self.name, self.gen, self.deps, self.prio = name, gen, deps, prio
                self.ready = 0.0
                self.done = False
                self.started = False

        tasks = []
        by = {}

        def add(name, genf, deps, prio, c=None):
            t = Task(name, genf, [d for d in deps if d is not None], prio)
            t.c = c
            tasks.append(t)
            by[name] = t
            return t

        ng = len(ctxs)
        for gi, c in enumerate(ctxs):
            c.prev = ctxs[gi - 1] if gi > 0 else None
            c.a5_done = 0
        g_ = lambda n, i: by.get("%s%d" % (n, i))
        for gi, c in enumerate(ctxs):
            add("P0%d" % gi, (lambda c=c: gen_P0(c)), [g_("Q", gi - 1), g_("C", gi - 1)], 3)
            early = (gi == 1)
            add("Q%d" % gi, (lambda c=c: gen_Q(c)),
                [g_("P0", gi), g_("R", gi - 1), g_("C", gi - 1) if early else g_("B3", gi - 1)], 4, c)
            add("R%d" % gi, (lambda c=c: gen_R(c)), [g_("Q", gi), g_("B3", gi - 1) if early else None], 6)
            add("C%d" % gi, (lambda c=c: gen_conv(c)), [g_("Q", gi), g_("B3", gi - 1) if early else None], 5, c)
            add("B3%d" % gi, (lambda c=c: gen_B3(c)), [g_("R", gi), g_("C", gi), g_("A4", gi - 1)], 7)
            add("A4%d" % gi, (lambda c=c: gen_phase4(c)), [g_("B3", gi), g_("A5", gi - 1)], 1)
            add("A5%d" % gi, (lambda c=c: gen_phase5(c)), [g_("A4", gi)], 5.5)

        clock = 0.0
        n_emitted = 0
        while True:
            active = [t for t in tasks if not t.done and all(d.done for d in t.deps)]
            if not active:
                break
            ready = [t for t in active if t.ready <= clock]
            order = sorted(ready, key=lambda t: -t.prio) + sorted(
                [t for t in active if t.ready > clock], key=lambda t: t.ready)
            progressed = False
            for t in order:
                if not t.started:
                    use_dve[0] = (clock < 330.0)
                    t.gen = t.gen()
                    t.started = True
                use_dve[0] = (clock < 330.0)
                try:
                    u = next(t.gen)
                except StopIteration:
                    t.done = True
                    progressed = True
                    break
                if u is None:
                    continue
                pe_t, dl = u
                if getattr(t, "c", None) is not None and t.c.sample:
                    pe_t, dl = 0.5, max(dl, 3.5)
                clock = max(clock, t.ready) + pe_t
                t.ready = clock + dl
                n_emitted += 1
                progressed = True
                break
            assert progressed, "scheduler deadlock"

        for d in d_y + [d_cv, d_rp, d_rs, d_ys, d_cvs]:
            pool.wait(d, d.count)
        P.emit(nc, st)
    return nc


_CACHE = {}


def kernel(x_prompt, x_sample, state_conv, state_ret, norm1_g, w_in, conv_w, ret_gn_g, w_out, norm2_g,
           w_gate, w_up, w_down, norm_f_g):
    f = lambda a: np.ascontiguousarray(np.asarray(a, dtype=np.float32))
    x_prompt, x_sample, state_conv, state_ret = f(x_prompt), f(x_sample), f(state_conv), f(state_ret)
    if "nc" not in _CACHE:
        _CACHE["nc"] = build_program()
        _CACHE["consts"] = _const_tables()
    nc = _CACHE["nc"]
    tabs, mask_p, mask_s, m1, m2, _, _ = _CACHE["consts"]
    w_in0 = f(w_in)[0]
    w_in_p = np.ascontiguousarray(np.concatenate(
        [w_in0[:, :2048]] + [w_in0[:, 2048 + 512 * k + 128 * cc:2048 + 512 * k + 128 * (cc + 1)]
                             for cc in range(4) for k in range(3)], axis=1))
    shared = {
        "w_in": w_in_p, "w_out": f(w_out)[0], "w_gate": f(w_gate)[0], "w_up": f(w_up)[0],
        "w_down": f(w_down)[0],
        "g1T": np.ascontiguousarray(f(norm1_g)[0].reshape(8, 128).T),
        "g2T": np.ascontiguousarray(f(norm2_g)[0].reshape(8, 128).T),
        "gngT": np.ascontiguousarray(f(ret_gn_g)[0].reshape(4, 128).T),
        "cwT": np.ascontiguousarray(f(conv_w)[0].reshape(3, 4, 128).transpose(2, 1, 0)),
        "gfb": np.ascontiguousarray(np.broadcast_to(f(norm_f_g)[None, :], (128, D))),
        "tabs": tabs, "mask_p": mask_p, "mask_s": mask_s, "m1": m1, "m2": m2,
        "ident": np.eye(128, dtype=np.float32),
    }
    in_maps = []
    for c in range(NCORES):
        xs = x_sample[c * DEC_PER:(c + 1) * DEC_PER].reshape(DEC_PER * DEC_SEQ, D)
        m = dict(shared)
        m["x"] = np.ascontiguousarray(np.concatenate([x_prompt[c], xs], 0))
        m["sconv"] = np.ascontiguousarray(state_conv[0, c * DEC_PER:(c + 1) * DEC_PER].reshape(2 * DEC_PER, CW))
        m["sret"] = np.ascontiguousarray(state_ret[0, c * DEC_PER:(c + 1) * DEC_PER])
        in_maps.append(m)
    res = run_bass_kernel_spmd(nc, in_maps, core_ids=list(range(NCORES)))
    R = res.results
    y_prompt = np.stack([R[c]["y"][:SEQ] for c in range(NCORES)], 0)
    y_sample = np.concatenate([R[c]["y"][SEQ:].reshape(DEC_PER, DEC_SEQ, D) for c in range(NCORES)], 0)
    convp = np.stack([R[c]["convp"] for c in range(NCORES)], 0)[None]
    retp = np.stack([R[c]["retp"] for c in range(NCORES)], 0)[None]
    convs = np.concatenate([R[c]["convs"].reshape(DEC_PER, 2, CW) for c in range(NCORES)], 0)[None]
    rets = np.concatenate([R[c]["rets"] for c in range(NCORES)], 0)[None]
    return (y_prompt.astype(np.float32), y_sample.astype(np.float32), convp.astype(np.float32),
            retp.astype(np.float32), convs.astype(np.float32), rets.astype(np.float32))
```

```python
import types
import numpy as np
from contextlib import ExitStack
import concourse.bass as bass
import concourse.mybir as mybir
from concourse.bass_utils import run_bass_kernel_spmd

F32 = mybir.dt.float32
BF16 = mybir.dt.bfloat16
AF = mybir.ActivationFunctionType
ALU = mybir.AluOpType

D = 1024
SEQ = 2048
NCORES = 8
DEC_PER = 16
DEC_SEQ = 8
PAST_LEN = 16384
H = 4
HD = 128
CW = 512
DFF = 2816
NFC = 22
INC = 3584
NT = 17
EPS = 1e-6
GN_EPS = 1e-5
NPAN_IN = 8
NPAN_GU = 11
PANW = 4096


def _freeze(fn):
    if fn.__closure__ is None:
        return fn
    cells = []
    for c in fn.__closure__:
        try:
            cells.append(types.CellType(c.cell_contents))
        except ValueError:
            cells.append(c)
    return types.FunctionType(fn.__code__, fn.__globals__, fn.__name__, fn.__defaults__, tuple(cells))


class Eng:
    def __init__(self, name):
        self.name = name
        self.ops = []
        self.count = 0
        self.waited = {}
        self.sem = None

    def wait(self, key, val):
        if key is None or val is None or val <= 0:
            return
        k = id(key)
        if self.waited.get(k, 0) >= val:
            return
        self.waited[k] = val
        self.ops.append(("wait", (key, val)))


class DmaSem:
    def __init__(self, name):
        self.name = name
        self.count = 0
        self.sem = None


class Buf:
    def __init__(self, name, excl=False):
        self.name = name
        self.w = None
        self.r = {}
        self.excl = excl

    def rdeps(self):
        d = [self.w] if self.w else []
        if self.excl:
            d += list(self.r.values())
        return d

    def wdeps(self):
        d = [self.w] if self.w else []
        return d + list(self.r.values())

    def note_r(self, tk):
        self.r[id(tk[0])] = tk

    def note_w(self, tk):
        self.w = tk
        self.r = {}


class Prog:
    def __init__(self):
        self.pe = Eng("pe")
        self.act = Eng("act")
        self.dve = Eng("dve")
        self.pool = Eng("pool")
        self.sp = Eng("sp")
        self.engs = [self.pe, self.act, self.dve, self.pool, self.sp]
        self.dsems = []

    def dsem(self, name):
        d = DmaSem(name)
        self.dsems.append(d)
        return d

    def _deps(self, eng, reads, writes, extra):
        deps = list(extra)
        for b in reads:
            deps += b.rdeps()
        for b in writes:
            deps += b.wdeps()
        for (k, v) in deps:
            eng.wait(k, v)

    def op(self, eng, fn, reads=(), writes=(), extra=()):
        self._deps(eng, reads, writes, extra)
        eng.count += 1
        eng.ops.append(("op", (_freeze(fn), True)))
        tk = (eng, eng.count)
        for b in reads:
            b.note_r(tk)
        for b in writes:
            b.note_w(tk)
        return tk

    def group(self, eng, fns, reads=(), writes=(), extra=()):
        self._deps(eng, reads, writes, extra)
        for fn in fns[:-1]:
            eng.ops.append(("op", (_freeze(fn), False)))
        eng.count += 1
        eng.ops.append(("op", (_freeze(fns[-1]), True)))
        tk = (eng, eng.count)
        for b in reads:
            b.note_r(tk)
        for b in writes:
            b.note_w(tk)
        return tk

    def dma(self, eng, fn, dsem, reads=(), writes=(), extra=()):
        self._deps(eng, reads, writes, extra)
        dsem.count += 16
        eng.ops.append(("dma", (_freeze(fn), dsem)))
        tk = (dsem, dsem.count)
        for b in reads:
            b.note_r(tk)
        for b in writes:
            b.note_w(tk)
        return tk

    def emit(self, nc, stack):
        for e in self.engs:
            e.sem = stack.enter_context(nc.semaphore("s_" + e.name))
        for d in self.dsems:
            d.sem = stack.enter_context(nc.semaphore("d_" + d.name))
        block = stack.enter_context(nc.Block())

        def run(ir):
            def body(e):
                for kind, payload in ir.ops:
                    if kind == "wait":
                        key, val = payload
                        e.wait_ge(key.sem, val)
                    elif kind == "op":
                        fn, signal = payload
                        ins = fn(e)
                        if signal:
                            ins.then_inc(ir.sem, 1)
                    else:
                        fn, dsem = payload
                        fn(e).then_inc(dsem.sem, 16)
            return body

        block.tensor(run(self.pe))
        block.scalar(run(self.act))
        block.vector(run(self.dve))
        block.gpsimd(run(self.pool))
        block.sync(run(self.sp))


def _const_tables():
    half = HD // 2
    inv = (np.float32(10000.0) ** (-(np.arange(half, dtype=np.float32)) / np.float32(half))).astype(np.float32)
    gam = 1.0 - 2.0 ** (-5.0 - np.arange(H, dtype=np.float64))
    tabs = np.zeros((NT, 128, 1536), np.float32)
    p = np.arange(128)
    for gt in range(NT):
        if gt < 16:
            pos = (128 * gt + p).astype(np.float32)
            loc = p.astype(np.float64)
        else:
            pos = (PAST_LEN + (p % DEC_SEQ)).astype(np.float32)
            loc = (p % DEC_SEQ).astype(np.float64)
        ang = (pos[:, None] * inv[None, :]).astype(np.float32).astype(np.float64)
        c = np.cos(ang)
        s = np.sin(ang)
        sq = gam[None, :] ** (loc[:, None] + 1.0)
        sk = gam[None, :] ** (-(loc[:, None] + 1.0)) * (HD ** -0.5)
        CQ = c[:, None, :] * sq[:, :, None]
        CK = c[:, None, :] * sk[:, :, None]
        SQ = np.stack([-s[:, None, :] * sq[:, :, None], s[:, None, :] * sq[:, :, None]], 2)
        SK = np.stack([-s[:, None, :] * sk[:, :, None], s[:, None, :] * sk[:, :, None]], 2)
        tabs[gt] = np.concatenate([CQ.reshape(128, 256), SQ.reshape(128, 512),
                                   CK.reshape(128, 256), SK.reshape(128, 512)], 1).astype(np.float32)
    j = p[:, None]
    i = p[None, :]
    mask_p = (j <= i).astype(np.float32)
    mask_s = ((j <= i) & (j // DEC_SEQ == i // DEC_SEQ)).astype(np.float32)
    m1 = (p[:, None] // DEC_SEQ == np.arange(DEC_PER)[None, :]).astype(np.float32)
    m2 = np.broadcast_to(m1.T[None, :, :], (128, DEC_PER, 128)).astype(np.float32)
    gC_p = (gam ** 128.0).astype(np.float64)
    gC_s = (gam ** float(DEC_SEQ)).astype(np.float64)
    return tabs, mask_p, mask_s, m1, np.ascontiguousarray(m2), gC_p, gC_s


def build_program(groups=None):
    tabs_np, mask_p_np, mask_s_np, m1_np, m2_np, gC_p, gC_s = _const_tables()
    nc = bass.Bass("TRN2", target_bir_lowering=False)
    P = Prog()

    def din(name, shape, dt=F32):
        return nc.dram_tensor(name, list(shape), dt, kind="ExternalInput").ap()

    def dout(name, shape, dt=F32):
        return nc.dram_tensor(name, list(shape), dt, kind="ExternalOutput").ap()

    x_d = din("x", [NT * 128, D])
    sc_d = din("sconv", [2 * DEC_PER, CW])
    sr_d = din("sret", [DEC_PER, H, HD, HD])
    w_in_d = din("w_in", [D, INC])
    w_out_d = din("w_out", [D, D])
    w_gate_d = din("w_gate", [D, DFF])
    w_up_d = din("w_up", [D, DFF])
    w_down_d = din("w_down", [DFF, D])
    g1T_d = din("g1T", [128, 8])
    g2T_d = din("g2T", [128, 8])
    gngT_d = din("gngT", [128, 4])
    cwT_d = din("cwT", [128, 4, 3])
    gfb_d = din("gfb", [128, D])
    tabs_d = din("tabs", [NT, 128, 1536])
    maskp_d = din("mask_p", [128, 128])
    masks_d = din("mask_s", [128, 128])
    m1_d = din("m1", [128, DEC_PER])
    m2_d = din("m2", [128, DEC_PER, 128])
    ident_d = din("ident", [128, 128])

    y_d = dout("y", [NT * 128, D])
    convp_d = dout("convp", [2, CW])
    retp_d = dout("retp", [H, HD, HD])
    convs_d = dout("convs", [2 * DEC_PER, CW])
    rets_d = dout("rets", [DEC_PER, H, HD, HD])

    ws_d = nc.dram_tensor("wscratch", [NPAN_IN + NPAN_GU + 2, 128, PANW], BF16, kind="Internal").ap()

    with ExitStack() as st:
        ARENA_W = 52600
        arena = st.enter_context(nc.sbuf_tensor("arena", [128, ARENA_W], F32))
        cur = [0]

        def alloc(shape, dt=F32):
            n = int(np.prod(shape))
            words = n if dt == F32 else (n + 1) // 2
            words = (words + 7) // 8 * 8
            off = cur[0]
            cur[0] += words
            assert cur[0] <= ARENA_W, ("SBUF arena overflow", cur[0])
            ap = arena[:, off:off + words]
            if dt != F32:
                ap = ap.bitcast(dt)[:, 0:n]
            else:
                ap = ap[:, 0:n]
            if len(shape) == 2:
                ap = ap.rearrange("p (a b) -> p a b", a=shape[0])
            elif len(shape) == 3:
                ap = ap.rearrange("p (a b c) -> p a b c", a=shape[0], b=shape[1])
            return ap

        wpan = [alloc([PANW], BF16) for _ in range(4)]
        wdn = alloc([NFC, D], BF16)
        xres = [alloc([D]) for _ in range(4)]
        xin = [alloc([D]) for _ in range(2)]
        hb0 = alloc([D], BF16)
        hb3 = alloc([D], BF16)
        h1T = alloc([8, 512], BF16)
        h2T = alloc([8, 512], BF16)
        oT = alloc([8, 512], BF16)
        Tst = alloc([H, HD])
        Rbf = alloc([H, HD], BF16)
        junk = alloc([D], BF16)
        gfb = alloc([D])
        sgg = [alloc([512], BF16) for _ in range(2)]
        maskp = alloc([128])
        masks = alloc([128])
        identf = alloc([128])
        identb = alloc([128], BF16)
        g1T = alloc([8])
        g2T = alloc([8])
        gngT = alloc([4])
        cwT = alloc([4, 3])
        mhalf = alloc([8])
        ss = alloc([8])
        ms = alloc([8])
        rstd = alloc([8])
        bnst = alloc([H, 6])
        bnag = alloc([H, 2])
        gvar = alloc([H])
        grstd = alloc([H])
        gnmr = alloc([H])
        m1c = alloc([DEC_PER])
        convT = alloc([32])
        convo = alloc([128])
        sc_tok = alloc([128])
        qk_tok = alloc([4, 2, 512], BF16)
        v_tok = alloc([4, 512], BF16)
        ra = alloc([512])
        rm = alloc([512])
        tabh = [alloc([768]) for _ in range(2)]
        qkT = alloc([8, 128], BF16)
        ST_sb = alloc([H, 128], BF16)
        on_sb = alloc([512], BF16)
        sgT = alloc([4, 512], BF16)
        Ccp = alloc([512])
        ubuf1 = alloc([2 + 512])
        halo = alloc([4, 2 * 1])
        Rf = alloc([DEC_PER // 2, HD])
        qf = alloc([H, 128])
        Vblk = rm.bitcast(BF16).rearrange("p (b v) -> p b v", b=DEC_PER // 2)
        aT = alloc([NFC, 512], BF16)

        banks = [st.enter_context(nc.psum_tensor("pb%d" % i, [128, 512], F32)) for i in range(7)]
        ptb = st.enter_context(nc.psum_tensor("ptb", [128, 1024], BF16))
        DBK = [0, 1, 2, 3]
        BS, BO, BI = 4, 5, 6
        bankB = [Buf("bank%d" % i, excl=True) for i in range(7)]
        ptB = Buf("ptb", excl=True)
        db_i = [0]

        def next_bank():
            i = DBK[db_i[0] % len(DBK)]
            db_i[0] += 1
            return banks[i], bankB[i]

        B = lambda n: Buf(n)
        wpanB = [B("wpan%d" % i) for i in range(4)]
        wdnB = B("wdn")
        xresB = [B("xres%d" % i) for i in range(4)]
        xinB = [B("xin0"), B("xin1")]
        hb0B, hb3B = B("hb0"), B("hb3")
        h1TB = [B("h1T%d" % i) for i in range(4)]
        h2TB = [B("h2T%d" % i) for i in range(4)]
        oTrB = [B("oTr%d" % i) for i in range(4)]
        oTcB = B("oTc")
        TstB, RbfB = B("Tst"), B("Rbf")
        junkB = B("junk")
        constB = B("const")
        sggB = [B("sgg0"), B("sgg1")]
        ssB, msB, rstdB = B("ss"), B("ms"), B("rstd")
        bnB, gnB = B("bn"), B("gn")
        qkB = [[B("qk%d_%d" % (s, j)) for j in range(2)] for s in range(4)]
        vB = [B("v%d" % s) for s in range(4)]
        raB, rmB = B("ra"), B("rm")
        tabB = [B("tab0"), B("tab1")]
        qkTB, STB, onB = B("qkT"), B("ST"), B("on")
        sgTB = B("sgT")
        CcpB = B("Ccp")
        uB1 = B("u")
        haloB = [B("halo%d" % i) for i in range(4)]
        aTB = [B("aT%d" % i) for i in range(NFC)]
        RfB, qfB = B("Rf"), B("qf")
        VblkB = rmB
        gvB, grB, gmB = B("gvar"), B("grstd"), B("gnmr")
        convTB, convoB, scB = B("convT"), B("convo"), B("sc")
        wsB = [B("ws%d" % i) for i in range(NPAN_IN + NPAN_GU + 2)]

        pe, act, dve, pool, sp = P.pe, P.act, P.dve, P.pool, P.sp

        d_const = P.dsem("const")
        d_ws = [P.dsem("ws%d" % i) for i in range(NPAN_IN + NPAN_GU + 2)]
        d_wpan = [P.dsem("wpan%d" % i) for i in range(4)]
        d_wdn = P.dsem("wdn")
        d_x = [P.dsem("x%d" % i) for i in range(4)]
        d_xin = [P.dsem("xin0"), P.dsem("xin1")]
        d_y = [P.dsem("y%d" % i) for i in range(4)]
        xin_i = [0]
        d_tab = [P.dsem("tab0"), P.dsem("tab1")]
        tab_i = [0]
        d_misc = P.dsem("misc")
        d_cv, d_rp = P.dsem("cv"), P.dsem("rp")
        d_rf, d_rs = P.dsem("rf"), P.dsem("rs")
        d_ys, d_cvs = P.dsem("ys"), P.dsem("cvs")

        def cload(dst, src):
            P.dma(sp, lambda e: e.dma_start(out=dst, in_=src), d_const, writes=[constB])

        cload(identf, ident_d[:, :])
        cload(maskp, maskp_d[:, :])
        cload(masks, masks_d[:, :])
        cload(g1T, g1T_d[:, :])
        cload(g2T, g2T_d[:, :])
        cload(gngT, gngT_d[:, :])
        cload(cwT, cwT_d[:, :, :])
        cload(gfb, gfb_d[:, :])
        cload(m1c, m1_d[:, :])
        identbB, mhB = B("identb"), B("mhalf")
        P.op(dve, lambda e: e.tensor_copy(out=identb, in_=identf), reads=[constB], writes=[identbB])
        P.op(pool, lambda e: e.memset(mhalf, -0.5), writes=[mhB])

        def cast_in_panel(j):
            dst = ws_d[j]
            if j < 4:
                P.dma(pool, lambda e: e.dma_start(
                    out=dst.rearrange("p (kc n) -> p kc n", kc=8),
                    in_=w_in_d[:, j * 512:(j + 1) * 512].rearrange("(kc p) n -> p kc n", p=128)),
                    d_ws[j], writes=[wsB[j]])
            else:
                cc = j - 4
                dv = dst[:, 0:8 * 384].rearrange("p (kc n) -> p kc n", kc=8)
                c0 = 2048 + 384 * cc
                P.dma(pool, lambda e: e.dma_start(
                    out=dv, in_=w_in_d[:, c0:c0 + 384].rearrange("(kc p) n -> p kc n", p=128)),
                    d_ws[j], writes=[wsB[j]])

        def cast_gu_panel(j):
            dst = ws_d[NPAN_IN + j].rearrange("p (kc t n) -> p kc t n", kc=8, t=2)
            for t, wd in enumerate((w_gate_d, w_up_d)):
                P.dma(pool, lambda e, t=t, wd=wd: e.dma_start(
                    out=dst[:, :, t, :],
                    in_=wd[:, j * 256:(j + 1) * 256].rearrange("(kc p) n -> p kc n", p=128)),
                    d_ws[NPAN_IN + j], writes=[wsB[NPAN_IN + j]])

        pool.wait(d_const, d_const.count)
        for j in range(NPAN_IN):
            cast_in_panel(j)
        for hh in range(2):
            jo = NPAN_IN + NPAN_GU + hh
            P.dma(pool, lambda e: e.dma_start(
                out=ws_d[jo].rearrange("p (kc n) -> p kc n", kc=8),
                in_=w_out_d[:, hh * 512:(hh + 1) * 512].rearrange("(kc p) n -> p kc n", p=128)),
                d_ws[jo], writes=[wsB[jo]])
        for j in range(NPAN_GU):
            cast_gu_panel(j)
        P.dma(pool, lambda e: e.dma_start(out=wdn, in_=w_down_d.rearrange("(fc p) n -> p fc n", p=128)),
              d_wdn, writes=[wdnB])
        pending_casts = []

        def issue_casts(n):
            for _ in range(n):
                if pending_casts:
                    pending_casts.pop(0)()

        pan_i = {"A": 0, "B": 0}

        def load_panel(idx, width=PANW, pool_="B"):
            k = pan_i[pool_] % 2 + (0 if pool_ == "A" else 2)
            pan_i[pool_] += 1
            P.dma(sp, lambda e: e.dma_start(out=wpan[k][:, 0:width], in_=ws_d[idx][:, 0:width]), d_wpan[k],
                  reads=[wsB[idx]], writes=[wpanB[k]])
            return wpan[k], wpanB[k]

        class PanelStream:
            def __init__(self, specs, pool_="B", ahead=True):
                self.specs, self.pool_, self.got, self.ahead = specs, pool_, {}, ahead

            def _ld(self, i):
                if i < len(self.specs) and i not in self.got:
                    idx, width = self.specs[i]
                    self.got[i] = load_panel(idx, width, self.pool_)

            def get(self, i):
                self._ld(i)
                if self.ahead:
                    self._ld(i + 1)
                return self.got[i]

        I32 = mybir.dt.int32
        nt1 = alloc([8])
        ntB = B("nt1")
        use_dve = [True]

        def rsq(x_ap, out_ap, xB, outB):
            if not use_dve[0]:
                k = x_ap.shape[1]
                P.op(pool, lambda e: e.tensor_tensor(out=out_ap, in0=x_ap, in1=mhalf[:, 0:k], op=ALU.pow),
                     reads=[xB, mhB], writes=[outB])
                return
            k = x_ap.shape[1]
            t1 = nt1[:, 0:k]
            P.op(dve, lambda e: e.tensor_scalar(out=out_ap.bitcast(I32), in0=x_ap.bitcast(I32), scalar1=-0.5,
                                                scalar2=float(0x5f3759df), op0=ALU.mult, op1=ALU.add),
                 reads=[xB], writes=[outB])
            for _ in range(2):
                if k == 1:
                    P.op(dve, lambda e: e.scalar_tensor_tensor(out=t1, in0=out_ap, scalar=x_ap[:, 0:1], in1=out_ap,
                                                               op0=ALU.mult, op1=ALU.mult),
                         reads=[outB, xB], writes=[ntB])
                else:
                    P.op(dve, lambda e: e.tensor_tensor(out=t1, in0=out_ap, in1=out_ap, op=ALU.mult),
                         reads=[outB], writes=[ntB])
                    P.op(dve, lambda e: e.tensor_tensor(out=t1, in0=t1, in1=x_ap, op=ALU.mult),
                         reads=[ntB, xB], writes=[ntB])
                P.op(dve, lambda e: e.tensor_scalar(out=t1, in0=t1, scalar1=-0.5, scalar2=1.5, op0=ALU.mult,
                                                    op1=ALU.add), reads=[ntB], writes=[ntB])
                P.op(dve, lambda e: e.tensor_tensor(out=out_ap, in0=out_ap, in1=t1, op=ALU.mult),
                     reads=[outB, ntB], writes=[outB])

        def rms_sq(src_ap, srcB, col):
            P.op(act, lambda e: e.activation(out=junk, in_=src_ap, func=AF.Square, scale=1.0 / 32.0,
                                             accum_out=ss[:, col:col + 1]),
                 reads=[srcB], writes=[ssB])

        def rms_rs(col):
            P.op(dve, lambda e: e.tensor_scalar_add(out=ms[:, col:col + 1], in0=ss[:, col:col + 1], scalar1=EPS),
                 reads=[ssB], writes=[msB])
            rsq(ms[:, col:col + 1], rstd[:, col:col + 1], msB, rstdB)

        def to_featmajor(s, gT, hT, hTB, hb, hbB):
            ptv = ptb[:, :].rearrange("p (a b) -> p a b", a=8)
            P.group(pe, [lambda e, kc=kc: e.transpose(out=ptv[:, kc, :], in_=hb[:, kc * 128:(kc + 1) * 128],
                                                     identity=identb) for kc in range(8)],
                    reads=[hbB, identbB], writes=[ptB])
            P.op(dve, lambda e: e.tensor_tensor(out=hT[:, :, s * 128:(s + 1) * 128], in0=ptv,
                                                in1=gT.unsqueeze(2).to_broadcast([128, 8, 128]), op=ALU.mult),
                 reads=[ptB, constB], writes=[hTB[s]])

        if groups is None:
            groups = [[0, 1, 2, 3], [4, 5, 6, 7], [8, 9, 10, 11], [12, 13, 14, 15], [16]]

        class Ctx:
            pass

        ctxs = []
        for gi, tiles in enumerate(groups):
            c = Ctx()
            c.gi, c.tiles, c.T, c.N = gi, tiles, len(tiles), 128 * len(tiles)
            c.sample = (tiles[0] == 16)
            c.nseq, c.L = (DEC_PER, DEC_SEQ) if c.sample else (1, c.N)
            c.gC = gC_s if c.sample else gC_p
            c.mask = masks if c.sample else maskp
            c.xs = list(range(c.T))
            ctxs.append(c)

        ptv = ptb[:, :].rearrange("p (a b) -> p a b", a=8)

        def phase0(c, s, part):
            gt = c.tiles[s]
            if part == "load":
                k = xin_i[0] % 2
                xin_i[0] += 1
                c.xk[s] = k
                P.dma(sp, lambda e: e.dma_start(out=xin[k], in_=x_d[gt * 128:(gt + 1) * 128, :]),
                      d_xin[k], writes=[xinB[k]])
            elif part == "sq":
                k = c.xk[s]
                rms_sq(xin[k], xinB[k], 0)
            elif part == 0:
                k = c.xk[s]
                rms_rs(0)
                P.op(dve, lambda e: e.tensor_scalar(out=hb0, in0=xin[k], scalar1=rstd[:, 0:1], scalar2=None,
                                                    op0=ALU.mult),
                     reads=[xinB[k], rstdB], writes=[hb0B])
            else:
                to_featmajor(s, g1T, h1T, h1TB, hb0, hb0B)

        def gen_phase1a(c):
            c.qstream = PanelStream([(0, PANW), (1, PANW), (2, PANW), (3, PANW)], ahead=c.sample)
            for blk in range(3):
                pan, panB = c.qstream.get(blk)
                panv = pan.rearrange("p (kc n) -> p kc n", kc=8)
                for s, gt in enumerate(c.tiles):
                    bk, bkB = next_bank()
                    P.group(pe, [lambda e, kc=kc: e.matmul(
                        bk[:, :], lhsT=h1T[:, kc, s * 128:(s + 1) * 128], rhs=panv[:, kc, :],
                        start=(kc == 0), stop=(kc == 7)) for kc in range(8)],
                        reads=[h1TB[s], panB], writes=[bkB])
                    if blk == 2:
                        P.op(act, lambda e: e.activation(out=v_tok[:, s, :], in_=bk[:, :], func=AF.Copy),
                             reads=[bkB], writes=[vB[s]])
                        yield (1.8, 0.0)
                        continue
                    c_off = 0 if blk == 0 else 768
                    tk_ = tab_i[0] % 2
                    tab_i[0] += 1
                    tab, tabB_ = tabh[tk_], tabB[tk_]
                    P.dma(sp, lambda e: e.dma_start(out=tab, in_=tabs_d[gt][:, c_off:c_off + 768]),
                          d_tab[tk_], writes=[tabB_])
                    Ct = tab[:, 0:256].rearrange("p (h d) -> p h d", h=4)
                    St = tab[:, 256:768].rearrange("p (h t d) -> p h t d", h=4, t=2)
                    ps4 = bk[:, :].rearrange("p (h t d) -> p h t d", h=4, t=2)
                    ra4 = ra.rearrange("p (h t d) -> p h t d", h=4, t=2)
                    rm4 = rm.rearrange("p (h t d) -> p h t d", h=4, t=2)
                    P.op(dve, lambda e: e.tensor_tensor(
                        out=ra4, in0=ps4, in1=Ct.unsqueeze(2).to_broadcast([128, 4, 2, 64]), op=ALU.mult),
                        reads=[bkB, tabB_], writes=[raB])
                    P.op(dve, lambda e: e.tensor_tensor(
                        out=rm4[:, :, 0, :], in0=ps4[:, :, 1, :], in1=St[:, :, 0, :], op=ALU.mult),
                        reads=[bkB, tabB_], writes=[rmB])
                    P.op(dve, lambda e: e.tensor_tensor(
                        out=rm4[:, :, 1, :], in0=ps4[:, :, 0, :], in1=St[:, :, 1, :], op=ALU.mult),
                        reads=[bkB, tabB_], writes=[rmB])
                    P.op(dve if use_dve[0] else pool,
                         lambda e: e.tensor_tensor(out=qk_tok[:, s, blk, :], in0=ra, in1=rm, op=ALU.add),
                         reads=[raB, rmB], writes=[qkB[s][blk]])
                    yield (1.8, 3.4)
                if c.gi == 0:
                    issue_casts(1)

        def gen_gpanel(c):
            N, T = c.N, c.T
            pan, panB = c.qstream.get(3)
            panv = pan.rearrange("p (kc n) -> p kc n", kc=8)
            for cc in range(4):
                bk, bkB = next_bank()
                P.group(pe, [lambda e, kc=kc: e.matmul(
                    bk[:, 0:N], lhsT=panv[:, kc, cc * 128:(cc + 1) * 128], rhs=h1T[:, kc, 0:N],
                    start=(kc == 0), stop=(kc == 7)) for kc in range(8)],
                    reads=h1TB[:T] + [panB], writes=[bkB])
                P.op(act, lambda e: e.activation(out=sgT[:, cc, 0:N], in_=bk[:, 0:N], func=AF.Silu),
                     reads=[bkB], writes=[sgTB])
                yield (1.8, 0.0)
            if c.gi == 0:
                issue_casts(1)

        def gen_conv(c):
            N, T, nseq, L, sample = c.N, c.T, c.nseq, c.L, c.sample
            v3 = lambda ap: ap.rearrange("p (b l) -> p b l", b=nseq)
            if sample:
                P.dma(sp, lambda e: e.dma_start(out=sc_tok[0:32, :], in_=sc_d[:, 0:128]), d_misc, writes=[scB])
            cstream = PanelStream([(4 + i, 8 * 384) for i in range(4)], ahead=sample)
            for cc in range(4):
                pan, panB = cstream.get(cc)
                panv = pan[:, 0:8 * 384].rearrange("p (kc n) -> p kc n", kc=8)
                u = ubuf1
                u3 = u[:, 0:nseq * (2 + L)].rearrange("p (b l) -> p b l", b=nseq)
                if c.tiles[0] == 0:
                    P.op(dve, lambda e: e.memset(u[:, 0:2], 0.0), writes=[uB1])
                elif not sample:
                    P.op(dve, lambda e: e.tensor_copy(out=u[:, 0:2], in_=halo[:, cc, :]),
                         reads=[haloB[cc]], writes=[uB1])
                if sample:
                    bk, bkB = next_bank()
                    P.op(pe, lambda e: e.matmul(
                        bk[:, 0:32], lhsT=sc_tok[0:32, :], rhs=identf[0:32, 0:32],
                        start=True, stop=True), reads=[scB, constB], writes=[bkB])
                    if cc + 1 < 4:
                        P.dma(sp, lambda e: e.dma_start(out=sc_tok[0:32, :],
                                                        in_=sc_d[:, (cc + 1) * 128:(cc + 2) * 128]),
                              d_misc, writes=[scB])
                    P.op(dve, lambda e: e.tensor_copy(
                        out=u3[:, :, 0:2], in_=bk[:, 0:32].rearrange("p (b j) -> p b j", b=DEC_PER)),
                        reads=[bkB], writes=[uB1])
                bks = {}
                for k in (1, 2, 0):
                    bk, bkB_ = next_bank()
                    bks[k] = (bk, bkB_)
                    P.group(pe, [lambda e, kc=kc: e.matmul(
                        bk[:, 0:N], lhsT=panv[:, kc, k * 128:(k + 1) * 128], rhs=h1T[:, kc, 0:N],
                        start=(kc == 0), stop=(kc == 7)) for kc in range(8)],
                        reads=h1TB[:T] + [panB], writes=[bkB_])
                    if k == 1:
                        bkC, bkCB = bk, bkB_
                        P.op(act, lambda e: e.activation(out=Ccp[:, 0:N], in_=bkC[:, 0:N], func=AF.Copy),
                             reads=[bkCB], writes=[CcpB])
                    elif k == 2:
                        bkX, bkXB = bk, bkB_
                        P.op(dve, lambda e: e.tensor_tensor(
                            out=u3[:, :, 2:2 + L], in0=v3(bkX[:, 0:N]), in1=v3(Ccp[:, 0:N]), op=ALU.mult),
                            reads=[bkXB, CcpB], writes=[uB1])
                        ca3 = v3(Ccp[:, 0:N])
                        P.op(dve, lambda e: e.tensor_scalar(
                            out=ca3, in0=u3[:, :, 2:2 + L], scalar1=cwT[:, cc, 2:3], scalar2=None, op0=ALU.mult),
                            reads=[uB1, constB], writes=[CcpB])
                        for jj in (1, 0):
                            P.op(dve, lambda e: e.scalar_tensor_tensor(
                                out=ca3, in0=u3[:, :, jj:jj + L], scalar=cwT[:, cc, jj:jj + 1], in1=ca3,
                                op0=ALU.mult, op1=ALU.add),
                                reads=[uB1, CcpB, constB], writes=[CcpB])
                    else:
                        bkG, bkGB = bk, bkB_
                        P.op(dve, lambda e: e.tensor_tensor(
                            out=oT[:, 4 + cc, 0:N], in0=bkG[:, 0:N], in1=Ccp[:, 0:N], op=ALU.mult),
                            reads=[bkGB, CcpB], writes=[oTcB])
                    yield (1.8, 2.0)
                if c.tiles[-1] == 15 or sample:
                    nr = 2 * nseq
                    P.op(dve, lambda e: e.tensor_copy(
                        out=convT[:, 0:nr].rearrange("p (b j) -> p b j", b=nseq), in_=u3[:, :, L:L + 2]),
                        reads=[uB1], writes=[convTB])
                    yield (0.0, 4.0)
                    bk, bkB = next_bank()
                    P.op(pe, lambda e: e.matmul(bk[0:nr, 0:128], lhsT=convT[:, 0:nr], rhs=identf,
                                                start=True, stop=True),
                         reads=[convTB, constB], writes=[bkB])
                    P.op(dve, lambda e: e.tensor_copy(out=convo[0:nr, :], in_=bk[0:nr, 0:128]),
                         reads=[bkB], writes=[convoB])
                    dst = convs_d if sample else convp_d
                    P.dma(sp if sample else pool,
                          lambda e: e.dma_start(out=dst[:, cc * 128:(cc + 1) * 128], in_=convo[0:nr, :]),
                          d_cvs if sample else d_cv, reads=[convoB])
                else:
                    P.op(dve, lambda e: e.tensor_copy(out=halo[:, cc, :], in_=u[:, N:N + 2]),
                         reads=[uB1], writes=[haloB[cc]])
                if c.gi == 0:
                    issue_casts(1)

        def gen_ret(c, s):
            gt = c.tiles[s]
            sample, gC, mask = c.sample, c.gC, c.mask
            first = (gt == 0)
            deferred_act = None
            P.group(pe, [lambda e, a=a: e.transpose(
                out=ptv[:, a, :], in_=qk_tok[:, s, a // 4, (a % 4) * 128:(a % 4 + 1) * 128], identity=identb)
                for a in range(8)],
                reads=[qkB[s][0], qkB[s][1], identbB], writes=[ptB])
            P.op(dve, lambda e: e.tensor_copy(out=qkT, in_=ptv), reads=[ptB], writes=[qkTB])
            yield (0.9, 2.5)
            Sv = banks[BS][:, :].rearrange("p (h i) -> p h i", h=4)
            P.group(pe, [lambda e, h=h: e.matmul(Sv[:, h, :], lhsT=qkT[:, 4 + h, :], rhs=qkT[:, h, :],
                                                 start=True, stop=True) for h in range(4)],
                    reads=[qkTB], writes=[bankB[BS]])
            P.op(dve, lambda e: e.tensor_tensor(
                out=ST_sb, in0=Sv, in1=mask.unsqueeze(1).to_broadcast([128, 4, 128]), op=ALU.mult),
                reads=[bankB[BS], constB], writes=[STB])
            yield (0.5, 2.5)
            Ov = banks[BO][:, :].rearrange("p (h v) -> p h v", h=4)
            Iv = banks[BI][:, :].rearrange("p (h v) -> p h v", h=4)
            if not sample:
                fns = []
                for h in range(4):
                    fns.append(lambda e, h=h: e.matmul(
                        Ov[:, h, :], lhsT=ST_sb[:, h, :], rhs=v_tok[:, s, h * 128:(h + 1) * 128],
                        start=True, stop=first))
                    if not first:
                        fns.append(lambda e, h=h: e.matmul(Ov[:, h, :], lhsT=qkT[:, h, :], rhs=Rbf[:, h, :],
                                                           start=False, stop=True))
                P.group(pe, fns, reads=[STB, vB[s], qkTB] + ([] if first else [RbfB]), writes=[bankB[BO]])
                P.group(pe, [lambda e, h=h: e.matmul(
                    Iv[:, h, :], lhsT=qk_tok[:, s, 1, h * 128:(h + 1) * 128],
                    rhs=v_tok[:, s, h * 128:(h + 1) * 128], start=True, stop=True) for h in range(4)],
                    reads=[qkB[s][1], vB[s]], writes=[bankB[BI]])
                yield (1.3, 1.2)
                if first:
                    P.op(dve, lambda e: e.tensor_copy(out=Tst, in_=Iv), reads=[bankB[BI]], writes=[TstB])
                else:
                    for h in range(4):
                        P.op(dve, lambda e, h=h: e.scalar_tensor_tensor(
                            out=Tst[:, h, :], in0=Tst[:, h, :], scalar=float(gC[h]), in1=Iv[:, h, :],
                            op0=ALU.mult, op1=ALU.add),
                            reads=[bankB[BI], TstB], writes=[TstB])
                if gt < 15:
                    def rbf_ops():
                        for h in range(4):
                            P.op(act, lambda e, h=h: e.activation(out=Rbf[:, h, :], in_=Tst[:, h, :], func=AF.Copy,
                                                                  scale=float(gC[h])),
                                 reads=[TstB], writes=[RbfB])
                    deferred_act = rbf_ops
                else:
                    for h in range(4):
                        P.op(act, lambda e, h=h: e.activation(out=Tst[:, h, :], in_=Tst[:, h, :], func=AF.Copy,
                                                              scale=float(gC[h])),
                             reads=[TstB], writes=[TstB])
                    P.dma(pool, lambda e: e.dma_start(out=retp_d.rearrange("h d v -> d h v"), in_=Tst),
                          d_rp, reads=[TstB])
            else:
                XTv = banks[BS][:, :].rearrange("p (h i) -> p h i", h=4)
                HB = DEC_PER // 2
                P.op(act, lambda e: e.activation(out=qf, in_=qkT[:, 0:4, :], func=AF.Copy),
                     reads=[qkTB], writes=[qfB])
                Vb = [(Vblk, VblkB), (ra.bitcast(BF16).rearrange("p (b v) -> p b v", b=HB), raB)]
                its = [(h, hf) for h in range(4) for hf in range(2)]

                def mk_vblk(i):
                    h_, hf_ = its[i]
                    vb, vbB = Vb[i % 2]
                    P.op(dve if use_dve[0] else pool, lambda e: e.tensor_tensor(
                        out=vb, in0=v_tok[:, s, h_ * 128:(h_ + 1) * 128].unsqueeze(1).to_broadcast([128, HB, 128]),
                        in1=m1c[:, hf_ * HB:(hf_ + 1) * HB].unsqueeze(2).to_broadcast([128, HB, 128]), op=ALU.mult),
                        reads=[vB[s], constB], writes=[vbB])

                def ld_rf(i):
                    h_, hf_ = its[i]
                    P.dma(pool, lambda e: e.dma_start(
                        out=Rf, in_=sr_d[hf_ * HB:(hf_ + 1) * HB, h_, :, :].rearrange("b d v -> d b v")),
                        d_rf, writes=[RfB])

                mk_vblk(0)
                ld_rf(0)
                for i, (h, hf) in enumerate(its):
                    b0 = hf * HB
                    vb, vbB = Vb[i % 2]
                    P.group(pe, [lambda e, bl=bl: e.matmul(
                        XTv[:, h, (b0 + bl) * DEC_SEQ:(b0 + bl + 1) * DEC_SEQ], lhsT=Rf[:, bl, :],
                        rhs=qf[:, h, (b0 + bl) * DEC_SEQ:(b0 + bl + 1) * DEC_SEQ], start=True, stop=True)
                        for bl in range(HB)],
                        reads=[RfB, qfB, STB], writes=[bankB[BS]])
                    bk2 = [next_bank(), next_bank()]
                    for q4 in range(2):
                        bk, bkB = bk2[q4]
                        P.op(pe, lambda e: e.matmul(
                            bk[:, :], lhsT=qk_tok[:, s, 1, h * 128:(h + 1) * 128],
                            rhs=vb[:, 4 * q4:4 * q4 + 4, :], start=True, stop=True),
                            reads=[qkB[s][1], vbB], writes=[bkB])
                    if i + 1 < len(its):
                        mk_vblk(i + 1)
                    P.op(dve, lambda e: e.tensor_scalar(out=Rf, in0=Rf, scalar1=float(gC[h]), scalar2=None,
                                                        op0=ALU.mult), reads=[RfB], writes=[RfB])
                    for q4 in range(2):
                        bk, bkB = bk2[q4]
                        P.op(dve, lambda e: e.scalar_tensor_tensor(
                            out=Rf[:, 4 * q4:4 * q4 + 4, :], in0=bk[:, :].rearrange("p (b v) -> p b v", b=4),
                            scalar=float(gC[h]), in1=Rf[:, 4 * q4:4 * q4 + 4, :], op0=ALU.mult, op1=ALU.add),
                            reads=[bkB, RfB], writes=[RfB])
                    P.dma(pool, lambda e: e.dma_start(
                        out=rets_d[b0:b0 + HB, h, :, :].rearrange("b d v -> d b v"), in_=Rf), d_rs, reads=[RfB])
                    if i + 1 < len(its):
                        ld_rf(i + 1)
                    yield (1.0, 8.0)
                P.op(act, lambda e: e.activation(out=ra, in_=banks[BS][:, :], func=AF.Copy),
                     reads=[bankB[BS]], writes=[raB])
                fns = []
                for h in range(4):
                    fns.append(lambda e, h=h: e.matmul(
                        Ov[:, h, :], lhsT=ST_sb[:, h, :], rhs=v_tok[:, s, h * 128:(h + 1) * 128],
                        start=True, stop=False))
                    fns.append(lambda e, h=h: e.matmul(
                        Ov[:, h, :], lhsT=ra[:, h * 128:(h + 1) * 128], rhs=identf, start=False, stop=True))
                P.group(pe, fns, reads=[STB, vB[s], raB, constB], writes=[bankB[BO]])
            pass
            for h in range(4):
                P.op(dve, lambda e, h=h: e.bn_stats(out=bnst[:, h, :], in_=Ov[:, h, :]),
                     reads=[bankB[BO]], writes=[bnB])
            for h in range(4):
                P.op(dve, lambda e, h=h: e.bn_aggr(out=bnag[:, h, :], in_=bnst[:, h, :]),
                     reads=[bnB], writes=[gnB])
            P.op(dve, lambda e: e.tensor_scalar_add(out=gvar, in0=bnag[:, :, 1], scalar1=GN_EPS),
                 reads=[gnB], writes=[gvB])
            yield (0.0 if not sample else 1.3, 3.5)
            rsq(gvar, grstd, gvB, grB)
            P.op(dve, lambda e: e.scalar_tensor_tensor(out=gnmr, in0=bnag[:, :, 0], scalar=-1.0, in1=grstd,
                                                       op0=ALU.mult, op1=ALU.mult),
                 reads=[gnB, grB], writes=[gmB])
            yield (0.0, 2.5)
            if deferred_act is not None:
                deferred_act()
            for h in range(4):
                P.op(act, lambda e, h=h: e.activation(
                    out=on_sb[:, h * 128:(h + 1) * 128], in_=Ov[:, h, :], func=AF.Identity,
                    scale=grstd[:, h:h + 1], bias=gnmr[:, h:h + 1]),
                    reads=[bankB[BO], grB, gmB], writes=[onB])
            yield (0.0, 4.0)
            P.group(pe, [lambda e, h=h: e.transpose(out=ptv[:, h, :], in_=on_sb[:, h * 128:(h + 1) * 128],
                                                    identity=identb) for h in range(4)],
                    reads=[onB, identbB], writes=[ptB])
            for h in range(4):
                P.op(dve, lambda e, h=h: e.scalar_tensor_tensor(
                    out=oT[:, h, s * 128:(s + 1) * 128], in0=ptv[:, h, :], scalar=gngT[:, h:h + 1],
                    in1=sgT[:, h, s * 128:(s + 1) * 128], op0=ALU.mult, op1=ALU.mult),
                    reads=[ptB, sgTB, constB], writes=[oTrB[s]])
            yield (0.5, 0.5)

        def gen_phase3(c, s, wo):
            xs = c.xs[s]
            gt = c.tiles[s]
            P.dma(sp, lambda e: e.dma_start(out=xres[xs], in_=x_d[gt * 128:(gt + 1) * 128, :]),
                  d_x[xs], writes=[xresB[xs]])
            for half in range(2):
                pan, panB = wo[half]
                panv = pan.rearrange("p (kc n) -> p kc n", kc=8)
                bk, bkB = next_bank()
                P.group(pe, [lambda e, kc=kc: e.matmul(
                    bk[:, :], lhsT=oT[:, kc, s * 128:(s + 1) * 128], rhs=panv[:, kc, :],
                    start=(kc == 0), stop=(kc == 7)) for kc in range(8)],
                    reads=[oTrB[s], oTcB, panB], writes=[bkB])
                P.op(dve, lambda e: e.tensor_tensor(
                    out=xres[xs][:, half * 512:(half + 1) * 512], in0=bk[:, :],
                    in1=xres[xs][:, half * 512:(half + 1) * 512], op=ALU.add),
                    reads=[bkB, xresB[xs]], writes=[xresB[xs]])
                yield (1.8, 0.0) if half == 0 else (1.8, 2.5)
            rms_sq(xres[xs], xresB[xs], 1)
            yield (0.0, 1.5)
            rms_rs(1)
            P.op(dve, lambda e: e.tensor_scalar(out=hb3, in0=xres[xs], scalar1=rstd[:, 1:2], scalar2=None,
                                                op0=ALU.mult),
                 reads=[xresB[xs], rstdB], writes=[hb3B])
            yield (0.0, 4.5)
            to_featmajor(s, g2T, h2T, h2TB, hb3, hb3B)
            yield (0.9, 1.2)

        def load_wout(pool_="B"):
            return [load_panel(NPAN_IN + NPAN_GU + hh, PANW, pool_) for hh in range(2)]

        def gen_phase4(c):
            N, T = c.N, c.T
            if c.sample:
                last_group = (c is ctxs[-1])
                for j in range(NPAN_GU):
                    pan, panB = load_panel(NPAN_IN + j, PANW, "B" if (last_group and (j // 2) % 2 == 1) else "A")
                    panv = pan.rearrange("p (kc t n) -> p kc t n", kc=8, t=2)
                    k2 = j % 2
                    bks = []
                    for t in range(2):
                        bk, bkB_ = next_bank()
                        bks.append((bk, bkB_))
                        P.group(pe, [lambda e, kc=kc: e.matmul(
                            bk[:, 0:256], lhsT=h2T[:, kc, 0:128], rhs=panv[:, kc, t, :],
                            start=(kc == 0), stop=(kc == 7)) for kc in range(8)],
                            reads=[h2TB[0], panB], writes=[bkB_])
                    (bkg, bkgB), (bku, bkuB) = bks
                    P.op(act, lambda e: e.activation(out=sgg[k2][:, 0:256], in_=bkg[:, 0:256], func=AF.Silu),
                         reads=[bkgB], writes=[sggB[k2]])
                    P.op(dve, lambda e: e.tensor_tensor(out=sgg[k2][:, 0:256], in0=bku[:, 0:256],
                                                        in1=sgg[k2][:, 0:256], op=ALU.mult),
                         reads=[bkuB, sggB[k2]], writes=[sggB[k2]])
                    yield (1.9, 2.5)
                    P.group(pe, [lambda e, fl=fl: e.transpose(
                        out=ptv[:, fl, :], in_=sgg[k2][:, fl * 128:(fl + 1) * 128], identity=identb)
                        for fl in range(2)], reads=[sggB[k2], identbB], writes=[ptB])
                    P.op(dve, lambda e: e.tensor_copy(out=aT[:, 2 * j:2 * j + 2, 0:128], in_=ptv[:, 0:2, :]),
                         reads=[ptB], writes=[aTB[2 * j], aTB[2 * j + 1]])
                    yield (0.3, 1.2)
                return
            for j in range(NPAN_GU):
                pan, panB = load_panel(NPAN_IN + j, PANW, "A")
                panv = pan.rearrange("p (kc t n) -> p kc t n", kc=8, t=2)
                for fl in range(2):
                    fc = 2 * j + fl
                    k2 = fc % 2
                    for t in range(2):
                        bk, bkB_ = next_bank()
                        P.group(pe, [lambda e, kc=kc: e.matmul(
                            bk[:, 0:N], lhsT=panv[:, kc, t, fl * 128:(fl + 1) * 128], rhs=h2T[:, kc, 0:N],
                            start=(kc == 0), stop=(kc == 7)) for kc in range(8)],
                            reads=h2TB[:T] + [panB], writes=[bkB_])
                        if t == 0:
                            P.op(act, lambda e: e.activation(out=sgg[k2][:, 0:N], in_=bk[:, 0:N], func=AF.Silu),
                                 reads=[bkB_], writes=[sggB[k2]])
                        else:
                            P.op(dve, lambda e: e.tensor_tensor(
                                out=aT[:, fc, 0:N], in0=bk[:, 0:N], in1=sgg[k2][:, 0:N], op=ALU.mult),
                                reads=[bkB_, sggB[k2]], writes=[aTB[fc]])
                        yield (0.45 * T, 0.0)

        def gen_phase5(c):
            for s, gt in enumerate(c.tiles):
                xs = c.xs[s]
                for half in range(2):
                    bk, bkB = next_bank()
                    P.group(pe, [lambda e, fc=fc: e.matmul(
                        bk[:, :], lhsT=aT[:, fc, s * 128:(s + 1) * 128], rhs=wdn[:, fc, half * 512:(half + 1) * 512],
                        start=(fc == 0), stop=(fc == NFC - 1)) for fc in range(NFC)],
                        reads=aTB + [wdnB], writes=[bkB])
                    P.op(dve, lambda e: e.tensor_tensor(
                        out=xres[xs][:, half * 512:(half + 1) * 512], in0=bk[:, :],
                        in1=xres[xs][:, half * 512:(half + 1) * 512], op=ALU.add),
                        reads=[bkB, xresB[xs]], writes=[xresB[xs]])
                    if half == 1:
                        yield (4.9, 1.5)
                        rms_sq(xres[xs], xresB[xs], 2)
                        yield (0.0, 1.5)
                        rms_rs(2)
                        P.op(dve, lambda e: e.scalar_tensor_tensor(
                            out=xres[xs], in0=xres[xs], scalar=rstd[:, 2:3], in1=gfb, op0=ALU.mult, op1=ALU.mult),
                            reads=[xresB[xs], rstdB, constB], writes=[xresB[xs]])
                        yield (0.0, 1.5)
                        P.dma(sp if c.sample else pool,
                              lambda e: e.dma_start(out=y_d[gt * 128:(gt + 1) * 128, :], in_=xres[xs]),
                              d_ys if c.sample else d_y[xs], reads=[xresB[xs]])
                        c.a5_done = s + 1
                        yield (0.0, 0.0)
                    else:
                        yield (4.9, 0.0)

        def step(g):
            try:
                next(g)
                return True
            except StopIteration:
                return False

        def gen_P0(c):
            c.xk = {}
            for s_ in range(c.T):
                phase0(c, s_, "load")
                yield (0.0, 3.5)
                phase0(c, s_, "sq")
                yield (0.0, 1.5)
                phase0(c, s_, 0)
                yield (0.0, 4.0)
                phase0(c, s_, 1)
                yield (0.9, 1.2)

        def gen_Q(c):
            for u in gen_phase1a(c):
                yield u
            for u in gen_gpanel(c):
                yield u

        def gen_R(c):
            for s_ in range(c.T):
                for u in gen_ret(c, s_):
                    yield u

        def gen_B3(c):
            wo = load_wout("A" if c.gi == 0 else "B")
            prev = c.prev
            for s_ in range(c.T):
                while prev is not None and s_ < prev.T and getattr(prev, "a5_done", 0) <= s_:
                    yield None
                for u in gen_phase3(c, s_, wo):
                    yield u

        class Task:
            def __init__(self, name, gen, deps, prio):
                self.name, self.gen, self.deps, self.prio = name, gen, deps, prio
                self.ready = 0.0
                self.done = False
                self.started = False

        tasks = []
        by = {}

        def add(name, genf, deps, prio, c=None):
            t = Task(name, genf, [d for d in deps if d is not None], prio)
            t.c = c
            tasks.append(t)
            by[name] = t
            return t

        ng = len(ctxs)
        for gi, c in enumerate(ctxs):
            c.prev = ctxs[gi - 1] if gi > 0 else None
            c.a5_done = 0
        g_ = lambda n, i: by.get("%s%d" % (n, i))
        for gi, c in enumerate(ctxs):
            add("P0%d" % gi, (lambda c=c: gen_P0(c)), [g_("Q", gi - 1), g_("C", gi - 1)], 3)
            early = (gi == 1)
            add("Q%d" % gi, (lambda c=c: gen_Q(c)),
                [g_("P0", gi), g_("R", gi - 1), g_("C", gi - 1) if early else g_("B3", gi - 1)], 4, c)
            add("R%d" % gi, (lambda c=c: gen_R(c)), [g_("Q", gi), g_("B3", gi - 1) if early else None], 6)
            add("C%d" % gi, (lambda c=c: gen_conv(c)), [g_("Q", gi), g_("B3", gi - 1) if early else None], 5, c)
            add("B3%d" % gi, (lambda c=c: gen_B3(c)), [g_("R", gi), g_("C", gi), g_("A4", gi - 1)], 7)
            add("A4%d" % gi, (lambda c=c: gen_phase4(c)), [g_("B3", gi), g_("A5", gi - 1)], 1)
            add("A5%d" % gi, (lambda c=c: gen_phase5(c)), [g_("A4", gi)], 5.5)

        clock = 0.0
        n_emitted = 0
        while True:
            active = [t for t in tasks if not t.done and all(d.done for d in t.deps)]
            if not active:
                break
            ready = [t for t in active if t.ready <= clock]
            order = sorted(ready, key=lambda t: -t.prio) + sorted(
                [t for t in active if t.ready > clock], key=lambda t: t.ready)
            progressed = False
            for t in order:
                if not t.started:
                    use_dve[0] = (clock < 330.0)
                    t.gen = t.gen()
                    t.started = True
                use_dve[0] = (clock < 330.0)
                try:
                    u = next(t.gen)
                except StopIteration:
                    t.done = True
                    progressed = True
                    break
                if u is None:
                    continue
                pe_t, dl = u
                if getattr(t, "c", None) is not None and t.c.sample:
                    pe_t, dl = 0.5, max(dl, 3.5)
                clock = max(clock, t.ready) + pe_t
                t.ready = clock + dl
                n_emitted += 1
                progressed = True
                break
            assert progressed, "scheduler deadlock"

        for d in d_y + [d_cv, d_rp, d_rs, d_ys, d_cvs]:
            pool.wait(d, d.count)
        P.emit(nc, st)
    return nc


_CACHE = {}


def kernel(x_prompt, x_sample, state_conv, state_ret, norm1_g, w_in, conv_w, ret_gn_g, w_out, norm2_g,
           w_gate, w_up, w_down, norm_f_g):
    f = lambda a: np.ascontiguousarray(np.asarray(a, dtype=np.float32))
    x_prompt, x_sample, state_conv, state_ret = f(x_prompt), f(x_sample), f(state_conv), f(state_ret)
    if "nc" not in _CACHE:
        _CACHE["nc"] = build_program()
        _CACHE["consts"] = _const_tables()
    nc = _CACHE["nc"]
    tabs, mask_p, mask_s, m1, m2, _, _ = _CACHE["consts"]
    w_in0 = f(w_in)[0]
    w_in_p = np.ascontiguousarray(np.concatenate(
        [w_in0[:, :2048]] + [w_in0[:, 2048 + 512 * k + 128 * cc:2048 + 512 * k + 128 * (cc + 1)]
                             for cc in range(4) for k in range(3)], axis=1))
    shared = {
        "w_in": w_in_p, "w_out": f(w_out)[0], "w_gate": f(w_gate)[0], "w_up": f(w_up)[0],
        "w_down": f(w_down)[0],
        "g1T": np.ascontiguousarray(f(norm1_g)[0].reshape(8, 128).T),
        "g2T": np.ascontiguousarray(f(norm2_g)[0].reshape(8, 128).T),
        "gngT": np.ascontiguousarray(f(ret_gn_g)[0].reshape(4, 128).T),
        "cwT": np.ascontiguousarray(f(conv_w)[0].reshape(3, 4, 128).transpose(2, 1, 0)),
        "gfb": np.ascontiguousarray(np.broadcast_to(f(norm_f_g)[None, :], (128, D))),
        "tabs": tabs, "mask_p": mask_p, "mask_s": mask_s, "m1": m1, "m2": m2,
        "ident": np.eye(128, dtype=np.float32),
    }
    in_maps = []
    for c in range(NCORES):
        xs = x_sample[c * DEC_PER:(c + 1) * DEC_PER].reshape(DEC_PER * DEC_SEQ, D)
        m = dict(shared)
        m["x"] = np.ascontiguousarray(np.concatenate([x_prompt[c], xs], 0))
        m["sconv"] = np.ascontiguousarray(state_conv[0, c * DEC_PER:(c + 1) * DEC_PER].reshape(2 * DEC_PER, CW))
        m["sret"] = np.ascontiguousarray(state_ret[0, c * DEC_PER:(c + 1) * DEC_PER])
        in_maps.append(m)
    res = run_bass_kernel_spmd(nc, in_maps, core_ids=list(range(NCORES)))
    R = res.results
    y_prompt = np.stack([R[c]["y"][:SEQ] for c in range(NCORES)], 0)
    y_sample = np.concatenate([R[c]["y"][SEQ:].reshape(DEC_PER, DEC_SEQ, D) for c in range(NCORES)], 0)
    convp = np.stack([R[c]["convp"] for c in range(NCORES)], 0)[None]
    retp = np.stack([R[c]["retp"] for c in range(NCORES)], 0)[None]
    convs = np.concatenate([R[c]["convs"].reshape(DEC_PER, 2, CW) for c in range(NCORES)], 0)[None]
    rets = np.concatenate([R[c]["rets"] for c in range(NCORES)], 0)[None]
    return (y_prompt.astype(np.float32), y_sample.astype(np.float32), convp.astype(np.float32),
            retp.astype(np.float32), convs.astype(np.float32), rets.astype(np.float32))
```
